# Optimizing a Trainium2 kernel written in Bass

```python
import jax, jax.numpy as jnp
from jax import lax
import numpy as np

D_MODEL = 2048
BATCH = 2
SEQ = 8192
DEPTH = 1

GRID_W = 64
HEAD_DIM = 128
N_HEADS = D_MODEL // HEAD_DIM
N_HEADS_NA = N_HEADS // 4
N_HEADS_DIL = N_HEADS - N_HEADS_NA
W_NA = N_HEADS_NA * HEAD_DIM
W_DIL = N_HEADS_DIL * HEAD_DIM
NA_ROWS = 8
NA_COLS = 16
NA_QCOLS = 16
NA_KCOLS = NA_QCOLS + NA_COLS
DIL_PAIRS = ((128, 1), (512, 4), (2048, 16))
D_FF = 5632
PLE_DIM = 256
ROPE_THETA = 10000.0
EPS = 1e-6
NEG = -1e30

kernel_name = "hymba_style_na_dilated_macaron_encoder"


def _rmsnorm(x, g):
    xf = x.astype(jnp.float32)
    y = xf * lax.rsqrt(jnp.mean(xf * xf, axis=-1, keepdims=True) + EPS) * g.astype(jnp.float32)
    return y.astype(x.dtype)


def _swiglu(u, w_gate, w_up, w_down):
    return (jax.nn.silu(u @ w_gate) * (u @ w_up)) @ w_down


def _rope(t):
    S, hd = t.shape[2], t.shape[3]
    inv = jnp.float32(ROPE_THETA) ** (-jnp.arange(0, hd, 2, dtype=jnp.float32) / hd)
    ang = jnp.arange(S, dtype=jnp.float32)[:, None] * inv[None, :]
    cos, sin = jnp.cos(ang), jnp.sin(ang)
    tf = t.astype(jnp.float32)
    t1, t2 = tf[..., : hd // 2], tf[..., hd // 2:]
    return jnp.concatenate([t1 * cos - t2 * sin, t2 * cos + t1 * sin], axis=-1).astype(t.dtype)


def _na_indices(rows):
    kr = min(NA_ROWS, rows)
    n_cb = GRID_W // NA_QCOLS
    r = np.arange(rows)
    rs = np.clip(r - kr // 2, 0, rows - kr)
    krow = rs[:, None] + np.arange(kr)[None, :]
    cb0 = np.arange(n_cb) * NA_QCOLS
    kcs = np.clip(cb0 - NA_COLS // 2, 0, GRID_W - NA_KCOLS)
    kcol = kcs[:, None] + np.arange(NA_KCOLS)[None, :]
    qcol = cb0[:, None] + np.arange(NA_QCOLS)[None, :]
    qs = np.clip(qcol - NA_COLS // 2, 0, GRID_W - NA_COLS)
    tok = krow[:, None, :, None] * GRID_W + kcol[None, :, None, :]
    col_ok = (kcol[:, None, :] >= qs[:, :, None]) & (kcol[:, None, :] < qs[:, :, None] + NA_COLS)
    col_ok = np.broadcast_to(col_ok[:, :, None, :], (n_cb, NA_QCOLS, kr, NA_KCOLS)).reshape(n_cb, NA_QCOLS, kr * NA_KCOLS)
    dr = krow - r[:, None] + NA_ROWS - 1
    dc = np.clip(kcol[:, None, :] - qcol[:, :, None] + NA_COLS - 1, 0, 2 * NA_COLS - 2)
    return kr, n_cb, tok, col_ok, dr, dc


def _neighborhood_attention(q, k, v, rpb):
    B, H, S, hd = q.shape
    rows = S // GRID_W
    kr, n_cb, tok, col_ok, dr, dc = _na_indices(rows)
    nk = kr * NA_KCOLS
    qb = q.reshape(B, H, rows, n_cb, NA_QCOLS, hd)
    flat = tok.reshape(-1)
    kb = jnp.take(k, flat, axis=2).reshape(B, H, rows, n_cb, nk, hd)
    vb = jnp.take(v, flat, axis=2).reshape(B, H, rows, n_cb, nk, hd)
    bias = rpb[:, dr[:, None, None, :, None], dc[None, :, :, None, :]]
    bias = bias.reshape(H, rows, n_cb, NA_QCOLS, nk).astype(jnp.float32)
    s = jnp.einsum('bhrcqd,bhrckd->bhrcqk', qb, kb).astype(jnp.float32) * (hd ** -0.5) + bias[None]
    s = jnp.where(col_ok, s, NEG)
    pr = jax.nn.softmax(s, axis=-1)
    o = jnp.einsum('bhrcqk,bhrckd->bhrcqd', pr.astype(v.dtype), vb)
    return o.reshape(B, H, S, hd)


def _dilated_branch(q, k, v, window, dil):
    B, H, S, hd = q.shape
    half = window // (2 * dil)
    blk = half
    L = S // dil
    nb = -(-L // blk)
    Lp = nb * blk

    def fold(t):
        return t.reshape(B, H, L, dil, hd).transpose(0, 1, 3, 2, 4)

    def kwin(t):
        tp = jnp.pad(t, ((0, 0), (0, 0), (0, 0), (blk, Lp - L + blk), (0, 0))).reshape(B, H, dil, nb + 2, blk, hd)
        return jnp.concatenate([tp[:, :, :, :-2], tp[:, :, :, 1:-1], tp[:, :, :, 2:]], axis=4)

    qb = jnp.pad(fold(q), ((0, 0), (0, 0), (0, 0), (0, Lp - L), (0, 0))).reshape(B, H, dil, nb, blk, hd)
    kb, vb = kwin(fold(k)), kwin(fold(v))
    m_q = np.arange(nb)[:, None] * blk + np.arange(blk)[None, :]
    m_k = np.arange(nb)[:, None] * blk - blk + np.arange(3 * blk)[None, :]
    mk = m_k[:, None, :]
    valid = (np.abs(mk - m_q[:, :, None]) <= half) & (mk >= 0) & (mk < L)
    s = jnp.einsum('bhrnqd,bhrnkd->bhrnqk', qb, kb).astype(jnp.float32) * (hd ** -0.5)
    s = jnp.where(valid, s, NEG)
    mx = jnp.max(s, axis=-1, keepdims=True)
    e = jnp.exp(s - mx)
    den = jnp.sum(e, axis=-1, keepdims=True)
    o = jnp.einsum('bhrnqk,bhrnkd->bhrnqd', e, vb.astype(jnp.float32)) / den
    lse = (mx + jnp.log(den))[..., 0]
    o = o.reshape(B, H, dil, Lp, hd)[:, :, :, :L].transpose(0, 1, 3, 2, 4).reshape(B, H, S, hd)
    lse = lse.reshape(B, H, dil, Lp)[:, :, :, :L].transpose(0, 1, 3, 2).reshape(B, H, S)
    return o, lse


def _dilated_mixture(q, k, v):
    res = [_dilated_branch(q, k, v, w, d) for (w, d) in DIL_PAIRS]
    o_all = jnp.stack([r[0] for r in res], axis=0)
    wts = jax.nn.softmax(jnp.stack([r[1] for r in res], axis=0), axis=0)
    return jnp.einsum('pbhs,pbhsd->bhsd', wts, o_all).astype(q.dtype)


def _mixer(u, w_qkv, na_rpb, out_g, w_o):
    B, S, _ = u.shape
    qkv = u @ w_qkv
    cuts = np.cumsum([W_NA, W_NA, W_NA, W_DIL, W_DIL])
    qa, ka, va, qd, kd, vd = jnp.split(qkv, cuts, axis=-1)

    def heads(t):
        return t.reshape(B, S, -1, HEAD_DIM).transpose(0, 2, 1, 3)

    o_na = _neighborhood_attention(heads(qa), heads(ka), heads(va), na_rpb)
    o_dil = _dilated_mixture(_rope(heads(qd)), _rope(heads(kd)), heads(vd))
    o = jnp.concatenate([o_na, o_dil], axis=1).astype(jnp.float32)
    o = o * lax.rsqrt(jnp.mean(o * o, axis=-1, keepdims=True) + EPS)
    o = o.transpose(0, 2, 1, 3).reshape(B, S, D_MODEL) * out_g.astype(jnp.float32)
    return o.astype(u.dtype) @ w_o


def setup_inputs(seed: int = 0) -> dict:
    key = jax.random.key(seed)
    ks = jax.random.split(key, 24)
    D = D_MODEL
    f32 = jnp.float32

    def nrm(k, shape, scale):
        return jax.random.normal(k, shape, f32) * scale

    def gain(k):
        return 1.0 + 0.05 * jax.random.normal(k, (DEPTH, D), f32)

    return {
        "x": nrm(ks[0], (BATCH, SEQ, D), 1.0),
        "p": nrm(ks[1], (DEPTH, BATCH, SEQ, PLE_DIM), 1.0),
        "ffn1_pre_g": gain(ks[2]),
        "ffn1_w_gate": nrm(ks[3], (DEPTH, D, D_FF), D ** -0.5),
        "ffn1_w_up": nrm(ks[4], (DEPTH, D, D_FF), D ** -0.5),
        "ffn1_w_down": nrm(ks[5], (DEPTH, D_FF, D), D_FF ** -0.5),
        "ffn1_post_g": gain(ks[6]),
        "mix_pre_g": gain(ks[7]),
        "w_qkv": nrm(ks[8], (DEPTH, D, 3 * D), D ** -0.5),
        "na_rpb": nrm(ks[9], (DEPTH, N_HEADS_NA, 2 * NA_ROWS - 1, 2 * NA_COLS - 1), 0.1),
        "out_g": gain(ks[10]),
        "w_o": nrm(ks[11], (DEPTH, D, D), D ** -0.5),
        "mix_post_g": gain(ks[12]),
        "ffn2_pre_g": gain(ks[13]),
        "ffn2_w_gate": nrm(ks[14], (DEPTH, D, D_FF), D ** -0.5),
        "ffn2_w_up": nrm(ks[15], (DEPTH, D, D_FF), D ** -0.5),
        "ffn2_w_down": nrm(ks[16], (DEPTH, D_FF, D), D_FF ** -0.5),
        "ffn2_post_g": gain(ks[17]),
        "ple_pre_g": gain(ks[18]),
        "w_ple_gate": nrm(ks[19], (DEPTH, D, D), D ** -0.5),
        "w_ple_proj": nrm(ks[20], (DEPTH, PLE_DIM, D), PLE_DIM ** -0.5),
        "ple_post_g": gain(ks[21]),
    }


def reference(x, p, ffn1_pre_g, ffn1_w_gate, ffn1_w_up, ffn1_w_down, ffn1_post_g,
              mix_pre_g, w_qkv, na_rpb, out_g, w_o, mix_post_g,
              ffn2_pre_g, ffn2_w_gate, ffn2_w_up, ffn2_w_down, ffn2_post_g,
              ple_pre_g, w_ple_gate, w_ple_proj, ple_post_g):
    h = x
    for i in range(DEPTH):
        f = _swiglu(_rmsnorm(h, ffn1_pre_g[i]), ffn1_w_gate[i], ffn1_w_up[i], ffn1_w_down[i])
        h = h + 0.5 * _rmsnorm(f, ffn1_post_g[i])
        m = _mixer(_rmsnorm(h, mix_pre_g[i]), w_qkv[i], na_rpb[i], out_g[i], w_o[i])
        h = h + _rmsnorm(m, mix_post_g[i])
        f = _swiglu(_rmsnorm(h, ffn2_pre_g[i]), ffn2_w_gate[i], ffn2_w_up[i], ffn2_w_down[i])
        h = h + 0.5 * _rmsnorm(f, ffn2_post_g[i])
        g = jax.nn.sigmoid(_rmsnorm(h, ple_pre_g[i]) @ w_ple_gate[i])
        h = h + _rmsnorm(g * (p[i] @ w_ple_proj[i]), ple_post_g[i])
    return h
```

```python
import numpy as np
import concourse.bass as bass
import concourse.mybir as mybir
from concourse.bass_utils import run_bass_kernel_spmd

F32 = mybir.dt.float32
BF16 = mybir.dt.bfloat16
AF = mybir.ActivationFunctionType
ALU = mybir.AluOpType

ENGS = ("pe", "act", "dve", "pool", "sp")

D = 2048
DFF = 5632
NJ = DFF // 128
SEQ = 8192
NTOK = 4096
OWN0 = 1024
NOWN = 2048
T = 512
EPS = 1e-6
MASKV = -30000.0
SB_BASE = 16512
SB_CAP = 212800


class Buf:
    __slots__ = ("name", "w", "r", "sem", "semcnt")

    def __init__(self, name):
        self.name = name
        self.w = None
        self.r = {}
        self.sem = None
        self.semcnt = 0


class Prog:
    def __init__(self, nc):
        self.nc = nc
        self.ops = {e: [] for e in ENGS}
        self.cnt = {e: 0 for e in ENGS}
        self.sems = {}
        self.seen = {e: {} for e in ENGS}
        self._stack = []
        self.dma_bufs = []
        self.sb_off = SB_BASE
        self.nalloc = 0

    def new_sem(self, name):
        self.nsem = getattr(self, "nsem", 0) + 1
        cm = self.nc.semaphore("%s_%d" % (name, self.nsem))
        s = cm.__enter__()
        self._stack.append(cm)
        return s

    def eng_sem(self, e):
        if e not in self.sems:
            self.sems[e] = self.new_sem("s_" + e)
        return self.sems[e]

    def sb(self, name, shape, dtype):
        nbytes = int(np.prod(shape[1:])) * (4 if dtype == F32 else 2)
        nbytes = (nbytes + 63) // 64 * 64
        off = self.sb_off
        assert off + nbytes <= SB_BASE + SB_CAP, ("SBUF overflow", name, off + nbytes - SB_BASE)
        self.sb_off += nbytes
        self.nalloc += 1
        return self.nc.alloc_sbuf_tensor_at("%s_%d" % (name, self.nalloc), list(shape), dtype, offset=off)

    def sb_mark(self):
        return self.sb_off

    def sb_reset(self, mark):
        self.sb_off = mark

    def _deps(self, eng, reads, writes):
        need = {}

        def add(ev):
            if ev is None:
                return
            k, c = ev
            if need.get(k, 0) < c:
                need[k] = c
        for b in reads:
            add(b.w)
        for b in writes:
            add(b.w)
            for k, c in b.r.items():
                add((k, c))
        waits = []
        seen = self.seen[eng]
        for k, c in need.items():
            if k == "pe" and eng == "pe":
                continue
            if seen.get(k, 0) >= c:
                continue
            seen[k] = c
            waits.append((k, c))
        return waits

    def _sem_of(self, k):
        return self.eng_sem(k) if isinstance(k, str) else k

    def op(self, eng, fn, reads=(), writes=(), inc=True):
        waits = [(self._sem_of(k), c) for k, c in self._deps(eng, reads, writes)]
        if inc:
            self.cnt[eng] += 1
            ev = (eng, self.cnt[eng])
            sem = self.eng_sem(eng)
        else:
            ev = (eng, self.cnt[eng] + 1)
            sem = None

        def run(e, waits=waits, fn=fn, sem=sem):
            for s, c in waits:
                e.wait_ge(s, c)
            ins = fn(e)
            if sem is not None:
                ins.then_inc(sem, 1)
        self.ops[eng].append(run)
        for b in reads:
            if b.r.get(ev[0], 0) < ev[1]:
                b.r[ev[0]] = ev[1]
        for b in writes:
            b.w = ev
            b.r = {}
        return ev

    def dma(self, q, fn, owner, reads=(), writes=()):
        waits = [(self._sem_of(k), c) for k, c in self._deps(q, reads, writes)]
        if owner.sem is None:
            owner.sem = self.new_sem("d_" + owner.name)
            self.dma_bufs.append(owner)
        owner.semcnt += 16
        sem, c = owner.sem, owner.semcnt

        def run(e, waits=waits, fn=fn, sem=sem):
            for s, cc in waits:
                e.wait_ge(s, cc)
            fn(e).then_inc(sem, 16)
        self.ops[q].append(run)
        ev = (sem, c)
        for b in reads:
            if b.r.get(sem, 0) < c:
                b.r[sem] = c
        for b in writes:
            b.w = ev
            b.r = {}
        return ev

    def barrier(self):
        evs = [(e, self.cnt[e]) for e in ENGS if self.cnt[e] > 0]
        evs += [(b.sem, b.semcnt) for b in self.dma_bufs if b.semcnt > 0]
        for eng in ENGS:
            waits = []
            for k, c in evs:
                if k == eng:
                    continue
                if self.seen[eng].get(k, 0) >= c:
                    continue
                self.seen[eng][k] = c
                waits.append((self._sem_of(k), c))

            def run(e, waits=waits):
                for s, c in waits:
                    e.wait_ge(s, c)
            self.ops[eng].append(run)

    def emit(self):
        nc = self.nc
        with nc.Block() as block:
            @block.tensor
            def _(e):
                for f in self.ops["pe"]:
                    f(e)

            @block.scalar
            def _(e):
                for f in self.ops["act"]:
                    f(e)

            @block.vector
            def _(e):
                for f in self.ops["dve"]:
                    f(e)

            @block.gpsimd
            def _(e):
                for f in self.ops["pool"]:
                    f(e)

            @block.sync
            def _(e):
                for f in self.ops["sp"]:
                    f(e)

    def close(self):
        while self._stack:
            self._stack.pop().__exit__(None, None, None)


def build_program(stop_phase=4, dbg=False):
    nc = bass.Bass("TRN2", target_bir_lowering=False)
    P = Prog(nc)
    r0 = nc.bump_sbuf(SB_CAP)
    assert r0 is not None and r0[0] == SB_BASE, r0

    def din(name, shape, dt=F32):
        return nc.dram_tensor(name, list(shape), dt, kind="ExternalInput").ap()

    def dscr(name, shape, dt):
        kind = "ExternalOutput" if dbg else "Internal"
        return nc.dram_tensor(name, list(shape), dt, kind=kind).ap()

    x_d = din("x", [NTOK, D])
    p_d = din("p", [NOWN, 256])
    wg_d = [din("wg1", [NJ, 128, 16, 128]), din("wg2", [NJ, 128, 16, 128])]
    wu_d = [din("wu1", [NJ, 128, 16, 128]), din("wu2", [NJ, 128, 16, 128])]
    wd_d = [din("wd1", [4, 4, 128, 11, 512]), din("wd2", [4, 4, 128, 11, 512])]
    wqkv_d = din("wqkv", [48, 128, 16, 128])
    wo_d = din("wo", [4, 1, 128, 16, 512])
    wpg_d = din("wpg", [4, 1, 128, 16, 512])
    wpp_d = din("wpp", [4, 1, 128, 2, 512])
    gains_d = din("gains", [8, D])
    outg_d = din("outg", [128, 16])
    consts_d = din("consts", [3, 128, 128])
    cos_d = din("cos", [128, NTOK])
    sin_d = din("sin", [128, NTOK])
    mdil_d = din("mdil", [128, 3 * 384 + 256])
    natab_d = din("natab", [4, 128, 5 * 896])
    out_d = nc.dram_tensor("out", [NOWN, D], F32, kind="ExternalOutput").ap()

    h1_d = dscr("h1s", [NOWN, D], F32)
    h2_d = dscr("h2s", [NOWN, D], F32)
    h3_d = dscr("h3s", [NOWN, D], F32)
    u2T_d = dscr("u2Ts", [16, 128, NTOK], BF16)
    qT_d = dscr("qTs", [16, 128, NOWN], BF16)
    kT_d = dscr("kTs", [16, 128, NTOK], BF16)
    vT_d = dscr("vTs", [16, 128, NTOK], BF16)
    oT_d = dscr("oTs", [16, 128, NOWN], BF16)

    def dint(name, shape, dt):
        return nc.dram_tensor(name, list(shape), dt, kind="Internal").ap()
    wg_c = [dint("wg1c", [NJ, 128, 16, 128], BF16), dint("wg2c", [NJ, 128, 16, 128], BF16)]
    wu_c = [dint("wu1c", [NJ, 128, 16, 128], BF16), dint("wu2c", [NJ, 128, 16, 128], BF16)]
    wd_c = [dint("wd1c", [4, 4, 128, 11, 512], BF16), dint("wd2c", [4, 4, 128, 11, 512], BF16)]
    wo_c = dint("woc", [4, 1, 128, 16, 512], BF16)
    wpg_c = dint("wpgc", [4, 1, 128, 16, 512], BF16)
    wpp_c = dint("wppc", [4, 1, 128, 2, 512], BF16)
    cache_bufs = {}

    def cbuf(key):
        if key not in cache_bufs:
            cache_bufs[key] = Buf("wc%d" % len(cache_bufs))
        return cache_bufs[key]

    def wload(dst_ap, slot_buf, w32_ap, wc_ap, key, first):
        if first:
            P.dma("pool", lambda e: e.dma_start(out=dst_ap, in_=w32_ap), slot_buf, writes=[slot_buf])
            P.dma("sp", lambda e: e.dma_start(out=wc_ap, in_=dst_ap), slot_buf, reads=[slot_buf], writes=[cbuf(key)])
        else:
            P.dma("pool", lambda e: e.dma_start(out=dst_ap, in_=wc_ap), slot_buf, reads=[cbuf(key)],
                  writes=[slot_buf])

    b_h1 = [Buf("h1_%d" % i) for i in range(16)]
    b_h2 = [Buf("h2_%d" % i) for i in range(16)]
    b_h3 = [Buf("h3_%d" % i) for i in range(16)]
    b_out = [Buf("out_%d" % i) for i in range(16)]
    b_u2T = [Buf("u2T_%d" % i) for i in range(8)]
    b_qT = [Buf("qT_%d" % i) for i in range(16)]
    b_kT = [Buf("kT_%d" % i) for i in range(16)]
    b_vT = [Buf("vT_%d" % i) for i in range(16)]
    b_oT = [Buf("oT_%d" % i) for i in range(16)]

    ps = nc.alloc_psum_tensor("ps", [128, 8, 512], F32)
    b_ps = [Buf("ps%d" % i) for i in range(8)]

    def psb(bank, n=1):
        return ps[:, bank:bank + n, :].bitcast(BF16).rearrange("p a b -> p (a b)")

    ident_b = P.sb("ident_b", [128, 128], BF16)
    rot_b = P.sb("rot_b", [128, 128], BF16)
    ones_b = P.sb("ones_b", [128, 128], BF16)
    eps_t = P.sb("eps_t", [128, 1], F32)
    stat = P.sb("stat", [128, 16], F32)
    b_const = Buf("const")
    b_eps = Buf("eps")
    P.dma("pool", lambda e: e.dma_start(out=ident_b[:], in_=consts_d[0]), b_const, writes=[b_const])
    P.dma("pool", lambda e: e.dma_start(out=rot_b[:], in_=consts_d[1]), b_const, writes=[b_const])
    P.dma("pool", lambda e: e.dma_start(out=ones_b[:], in_=consts_d[2]), b_const, writes=[b_const])
    P.op("dve", lambda e: e.memset(eps_t[:], EPS), writes=[b_eps])
    b_stat = [Buf("stat%d" % i) for i in range(4)]
    arena0 = P.sb_mark()

    class FFNBufs:
        pass

    def alloc_ffn_bufs():
        B = FFNBufs()
        B.xin = [P.sb("xin", [128, D], F32) for _ in range(2)]
        B.b_xin = [Buf("xin%d" % i) for i in range(2)]
        B.ft = [P.sb("ft", [128, D], F32) for _ in range(4)]
        B.b_ft = [Buf("ft%d" % i) for i in range(4)]
        B.xn = P.sb("xn", [128, D], BF16)
        B.b_xn = Buf("xn")
        B.junk = P.sb("junk", [128, D], BF16)
        B.b_junk = Buf("junk")
        B.uT = P.sb("uT", [128, 16, T], BF16)
        B.b_uT = [Buf("uT%d" % i) for i in range(4)]
        B.aT = P.sb("aT", [128, NJ, T], BF16)
        B.b_aT = [Buf("aT%d" % i) for i in range(NJ)]
        B.wgu = [P.sb("wgu", [128, 2, 16, 128], BF16) for _ in range(3)]
        B.b_wgu = [Buf("wgu%d" % i) for i in range(3)]
        B.wmv = [P.sb("wmv", [128, 16, 512], BF16) for _ in range(2)]
        B.b_wmv = [Buf("wmv%d" % i) for i in range(2)]
        B.gpre = P.sb("gpre", [128, D], F32)
        B.b_gpre = Buf("gpre")
        B.gpost = P.sb("gpost", [128, D], F32)
        B.b_gpost = Buf("gpost")
        B.sg = [P.sb("sg", [128, 512], F32) for _ in range(2)]
        B.b_sg = [Buf("sg%d" % i) for i in range(2)]
        B.wmv_n = 0
        B.wgu_n = 0
        return B

    def load_gain(B, which, idx):
        t, b = (B.gpre, B.b_gpre) if which == "pre" else (B.gpost, B.b_gpost)
        P.dma("sp", lambda e: e.dma_start(out=t[:], in_=gains_d[idx:idx + 1, :].partition_broadcast(128)),
              b, writes=[b])

    def rms_rstd(B, src, b_src, si):
        st = stat[:, si:si + 1]
        P.op("act", lambda e: e.activation(out=B.junk[:], in_=src[:], func=AF.Square,
                                           scale=float(D ** -0.5), accum_out=st),
             reads=[b_src], writes=[B.b_junk, b_stat[si]])
        P.op("act", lambda e: e.activation(out=st, in_=st, func=AF.Sqrt, bias=eps_t[:, 0:1], scale=1.0),
             reads=[b_stat[si], b_eps], writes=[b_stat[si]])
        P.op("dve", lambda e: e.reciprocal(out=st, in_=st), reads=[b_stat[si]], writes=[b_stat[si]])
        return st

    def norm_transpose(B, src, b_src, i, si=0):
        st = rms_rstd(B, src, b_src, si)
        P.op("dve", lambda e: e.scalar_tensor_tensor(out=B.xn[:], in0=src[:], scalar=st, in1=B.gpre[:],
                                                     op0=ALU.mult, op1=ALU.mult),
             reads=[b_src, b_stat[si], B.b_gpre], writes=[B.b_xn])
        pT = psb(0, 2)
        for c in range(16):
            P.op("pe", lambda e, c=c: e.transpose(out=pT[:, c * 128:(c + 1) * 128],
                                                  in_=B.xn[:, c * 128:(c + 1) * 128], identity=ident_b[:]),
                 reads=[B.b_xn, b_const], writes=[b_ps[0], b_ps[1]], inc=(c == 15))
        pv = pT.rearrange("p (c t) -> p c t", c=16)
        P.op("act", lambda e: e.copy(out=B.uT[:, 0:8, i * 128:(i + 1) * 128], in_=pv[:, 0:8, :]),
             reads=[b_ps[0]], writes=[B.b_uT[i]])
        P.op("dve", lambda e: e.tensor_copy(out=B.uT[:, 8:16, i * 128:(i + 1) * 128], in_=pv[:, 8:16, :]),
             reads=[b_ps[1]], writes=[B.b_uT[i]])

    def gemm_mov(B, w_d, njg, kg, lhs_fn, lhs_bufs_fn, evac_fn, w_c=None, wkey=None, first=True):
        for cb in range(4):
            for jg in range(njg):
                s = B.wmv_n % 2
                B.wmv_n += 1
                wload(B.wmv[s][:, 0:kg, :], B.b_wmv[s], w_d[cb, jg], w_c[cb, jg], (wkey, cb, jg), first)
                for i in range(4):
                    for jj in range(kg):
                        k = jg * kg + jj
                        P.op("pe", lambda e, s=s, i=i, jj=jj, k=k, jg=jg: e.matmul(
                            ps[:, 4 + i, :], lhsT=lhs_fn(k, i), rhs=B.wmv[s][:, jj, :],
                            start=(jg == 0 and jj == 0), stop=(jg == njg - 1 and jj == kg - 1)),
                            reads=[B.b_wmv[s]] + lhs_bufs_fn(k, i), writes=[b_ps[4 + i]], inc=(jj == kg - 1))
            for i in range(4):
                evac_fn(i, cb)

    def post_residual(B, i, resid_ap, b_resid, coef, si):
        st = rms_rstd(B, B.ft[i], B.b_ft[i], si)
        xs = i % 2
        P.dma("sp", lambda e: e.dma_start(out=B.xin[xs][:], in_=resid_ap), B.b_xin[xs],
              reads=[b_resid], writes=[B.b_xin[xs]])
        P.op("dve", lambda e: e.scalar_tensor_tensor(out=B.ft[i][:], in0=B.ft[i][:], scalar=st, in1=B.gpost[:],
                                                     op0=ALU.mult, op1=ALU.mult),
             reads=[B.b_ft[i], b_stat[si], B.b_gpost], writes=[B.b_ft[i]])
        P.op("dve", lambda e: e.scalar_tensor_tensor(out=B.ft[i][:], in0=B.ft[i][:], scalar=float(coef),
                                                     in1=B.xin[xs][:], op0=ALU.mult, op1=ALU.add),
             reads=[B.b_ft[i], B.b_xin[xs]], writes=[B.b_ft[i]])

    def ffn_stage(B, li, src_ap_fn, b_src_fn, gidx, first=True):
        load_gain(B, "pre", gidx)
        load_gain(B, "post", gidx + 1)
        for i in range(4):
            xs = i % 2
            P.dma("sp", lambda e, i=i, xs=xs: e.dma_start(out=B.xin[xs][:], in_=src_ap_fn(i)), B.b_xin[xs],
                  reads=[b_src_fn(i)], writes=[B.b_xin[xs]])
            norm_transpose(B, B.xin[xs], B.b_xin[xs], i, si=i % 2)
        for j in range(NJ):
            s = B.wgu_n % 3
            B.wgu_n += 1
            wload(B.wgu[s][:, 0], B.b_wgu[s], wg_d[li][j], wg_c[li][j], ("g", li, j), first)
            wload(B.wgu[s][:, 1], B.b_wgu[s], wu_d[li][j], wu_c[li][j], ("u", li, j), first)
            gb, ub = j % 2, 2 + j % 2
            for which, bank in ((0, gb), (1, ub)):
                for k in range(16):
                    P.op("pe", lambda e, s=s, which=which, bank=bank, k=k: e.matmul(
                        ps[:, bank, :], lhsT=B.wgu[s][:, which, k, :], rhs=B.uT[:, k, :],
                        start=(k == 0), stop=(k == 15)),
                        reads=[B.b_wgu[s]] + B.b_uT, writes=[b_ps[bank]], inc=(k == 15))
            sgs = j % 2
            P.op("act", lambda e, gb=gb, sgs=sgs: e.activation(out=B.sg[sgs][:], in_=ps[:, gb, :], func=AF.Silu),
                 reads=[b_ps[gb]], writes=[B.b_sg[sgs]])
            P.op("dve", lambda e, ub=ub, sgs=sgs, j=j: e.tensor_tensor(out=B.aT[:, j, :], in0=ps[:, ub, :],
                                                                      in1=B.sg[sgs][:], op=ALU.mult),
                 reads=[b_ps[ub], B.b_sg[sgs]], writes=[B.b_aT[j]])

        def evac(i, cb):
            eng = "act" if (i + cb) % 2 == 0 else "dve"
            if eng == "act":
                P.op("act", lambda e: e.copy(out=B.ft[i][:, cb * 512:(cb + 1) * 512], in_=ps[:, 4 + i, :]),
                     reads=[b_ps[4 + i]], writes=[B.b_ft[i]])
            else:
                P.op("dve", lambda e: e.tensor_copy(out=B.ft[i][:, cb * 512:(cb + 1) * 512], in_=ps[:, 4 + i, :]),
                     reads=[b_ps[4 + i]], writes=[B.b_ft[i]])
        gemm_mov(B, wd_d[li], 4, 11,
                 lambda k, i: B.aT[:, k, i * 128:(i + 1) * 128],
                 lambda k, i: [B.b_aT[k]], evac, w_c=wd_c[li], wkey=("d", li), first=first)
        for i in range(4):
            post_residual(B, i, src_ap_fn(i), b_src_fn(i), 0.5, si=i % 2)

    B = alloc_ffn_bufs()
    b_x = Buf("x_in")
    import os as _os
    n_tt1 = 0 if _os.environ.get('SKIP_P1') else NTOK // T
    for tt in range(n_tt1):
        ffn_stage(B, 0, lambda i, tt=tt: x_d[tt * T + i * 128: tt * T + (i + 1) * 128, :], lambda i: b_x, 0,
                  first=(tt == 0))
        load_gain(B, "pre", 2)
        for i in range(4):
            tok0 = tt * T + i * 128
            if OWN0 <= tok0 < OWN0 + NOWN:
                oi = (tok0 - OWN0) // 128
                P.dma("sp", lambda e, i=i, oi=oi: e.dma_start(out=h1_d[oi * 128:(oi + 1) * 128, :], in_=B.ft[i][:]),
                      B.b_ft[i], reads=[B.b_ft[i]], writes=[b_h1[oi]])
            norm_transpose(B, B.ft[i], B.b_ft[i], i, si=i % 2)
        P.dma("sp", lambda e, tt=tt: e.dma_start(
            out=u2T_d[:, :, tt * T:(tt + 1) * T].rearrange("k p t -> p k t"), in_=B.uT[:]),
            B.b_uT[0], reads=B.b_uT, writes=[b_u2T[tt]])
    if stop_phase <= 1:
        return finish(P, nc, [b_u2T[-1]] + b_h1)
    P.barrier()
    P.sb_reset(arena0)

    HT = NTOK // 2
    u2h = P.sb("u2h", [128, 16, HT], BF16)
    b_u2h = [Buf("u2h%d" % i) for i in range(4)]
    wq = [P.sb("wq", [128, 16, 128], BF16) for _ in range(3)]
    b_wq = [Buf("wq%d" % i) for i in range(3)]
    cosb = P.sb("cosb", [128, HT], F32)
    sinb = P.sb("sinb", [128, HT], F32)
    b_cs = Buf("cossin")
    ost = [P.sb("ost", [128, HT], BF16) for _ in range(2)]
    b_ost = [Buf("ost%d" % i) for i in range(2)]
    xb = [P.sb("xb", [128, 512], BF16) for _ in range(4)]
    b_xb = [Buf("xb%d" % i) for i in range(4)]
    t1 = [P.sb("t1", [128, 512], F32) for _ in range(4)]
    b_t1 = [Buf("t1_%d" % i) for i in range(4)]
    t2 = [P.sb("t2", [128, 512], F32) for _ in range(4)]
    b_t2 = [Buf("t2_%d" % i) for i in range(4)]
    chunk_map = []
    for c in range(48):
        if c < 4:
            chunk_map.append(("q", c, False))
        elif c < 8:
            chunk_map.append(("k", c - 4, False))
        elif c < 12:
            chunk_map.append(("v", c - 8, False))
        elif c < 24:
            chunk_map.append(("q", 4 + c - 12, True))
        elif c < 36:
            chunk_map.append(("k", 4 + c - 24, True))
        else:
            chunk_map.append(("v", 4 + c - 36, False))
    nwq = 0
    nost = 0
    nrope = 0
    npsq = 0
    import os as _os
    _nh = int(_os.environ.get('PH2_HALVES', '2'))
    _cl = [int(v) for v in _os.environ.get('PH2_CHUNKS', ','.join(map(str, range(48)))).split(',') if v != '']
    for half in range(_nh):
        for t4 in range(4):
            tt = half * 4 + t4
            P.dma("sp", lambda e, tt=tt, t4=t4: e.dma_start(
                out=u2h[:, :, t4 * T:(t4 + 1) * T], in_=u2T_d[:, :, tt * T:(tt + 1) * T].rearrange("k p t -> p k t")),
                b_u2h[t4], reads=[b_u2T[tt]], writes=[b_u2h[t4]])
        P.dma("sp", lambda e, half=half: e.dma_start(out=cosb[:], in_=cos_d[:, half * HT:(half + 1) * HT]),
              b_cs, writes=[b_cs])
        P.dma("sp", lambda e, half=half: e.dma_start(out=sinb[:], in_=sin_d[:, half * HT:(half + 1) * HT]),
              b_cs, writes=[b_cs])
        for c in _cl:
            kind, head, rope = chunk_map[c]
            s = nwq % 3
            nwq += 1
            P.dma("pool", lambda e, s=s, c=c: e.dma_start(out=wq[s][:], in_=wqkv_d[c]), b_wq[s], writes=[b_wq[s]])
            if kind == "q":
                tiles = [2, 3] if half == 0 else [0, 1]
            else:
                tiles = [0, 1, 2, 3]
            os_ = nost % 2
            nost += 1
            for t4 in tiles:
                bank = npsq % 4
                npsq += 1
                for k in range(16):
                    P.op("pe", lambda e, s=s, k=k, t4=t4, bank=bank: e.matmul(
                        ps[:, bank, :], lhsT=wq[s][:, k, :], rhs=u2h[:, k, t4 * T:(t4 + 1) * T],
                        start=(k == 0), stop=(k == 15)),
                        reads=[b_wq[s], b_u2h[t4]], writes=[b_ps[bank]], inc=(k == 15))
                dst = ost[os_][:, t4 * T:(t4 + 1) * T]
                if not rope and kind == "q":
                    P.op("act", lambda e, dst=dst, bank=bank: e.activation(out=dst, in_=ps[:, bank, :], func=AF.Copy,
                                                                           scale=float(128 ** -0.5)),
                         reads=[b_ps[bank]], writes=[b_ost[os_]])
                elif not rope:
                    if npsq % 2 == 0:
                        P.op("act", lambda e, dst=dst, bank=bank: e.copy(out=dst, in_=ps[:, bank, :]),
                             reads=[b_ps[bank]], writes=[b_ost[os_]])
                    else:
                        P.op("dve", lambda e, dst=dst, bank=bank: e.tensor_copy(out=dst, in_=ps[:, bank, :]),
                             reads=[b_ps[bank]], writes=[b_ost[os_]])
                else:
                    rs = nrope % 4
                    nrope += 1
                    rbank = 4 + rs
                    P.op("act", lambda e, rs=rs, bank=bank: e.copy(out=xb[rs][:], in_=ps[:, bank, :]),
                         reads=[b_ps[bank]], writes=[b_xb[rs]])
                    P.op("pe", lambda e, rs=rs, rbank=rbank: e.matmul(ps[:, rbank, :], lhsT=rot_b[:], rhs=xb[rs][:],
                                                                      start=True, stop=True),
                         reads=[b_xb[rs], b_const], writes=[b_ps[rbank]])
                    P.op("dve", lambda e, rs=rs, bank=bank, t4=t4: e.tensor_tensor(
                        out=t1[rs][:], in0=ps[:, bank, :], in1=cosb[:, t4 * T:(t4 + 1) * T], op=ALU.mult),
                        reads=[b_ps[bank], b_cs, b_xb[rs]], writes=[b_t1[rs]])
                    P.op("dve", lambda e, rs=rs, rbank=rbank, t4=t4: e.tensor_tensor(
                        out=t2[rs][:], in0=ps[:, rbank, :], in1=sinb[:, t4 * T:(t4 + 1) * T], op=ALU.mult),
                        reads=[b_ps[rbank], b_cs], writes=[b_t2[rs]])
                    P.op("dve", lambda e, rs=rs, dst=dst: e.tensor_tensor(out=dst, in0=t1[rs][:], in1=t2[rs][:],
                                                                          op=ALU.add),
                         reads=[b_t1[rs], b_t2[rs]], writes=[b_ost[os_]])
            if kind == "q":
                lo = tiles[0] * T
                dlo = 0 if half == 0 else NOWN // 2
                P.dma("sp", lambda e, os_=os_, head=head, lo=lo, dlo=dlo: e.dma_start(
                    out=qT_d[head, :, dlo:dlo + 2 * T], in_=ost[os_][:, lo:lo + 2 * T]),
                    b_ost[os_], reads=[b_ost[os_]], writes=[b_qT[head]])
            else:
                dd, bb = (kT_d, b_kT) if kind == "k" else (vT_d, b_vT)
                P.dma("sp", lambda e, os_=os_, head=head, dd=dd, half=half: e.dma_start(
                    out=dd[head, :, half * HT:(half + 1) * HT], in_=ost[os_][:]),
                    b_ost[os_], reads=[b_ost[os_]], writes=[bb[head]])
    if stop_phase <= 2:
        return finish(P, nc, b_qT + b_kT + b_vT + b_h1)
    P.barrier()
    P.sb_reset(arena0)

    qs_ = [P.sb("qs", [128, NOWN], BF16) for _ in range(2)]
    ks_ = [P.sb("ks", [128, NTOK], BF16) for _ in range(2)]
    vs_ = [P.sb("vs", [128, NTOK], BF16) for _ in range(2)]
    b_qs = [Buf("qs%d" % i) for i in range(2)]
    b_ks = [Buf("ks%d" % i) for i in range(2)]
    b_vs = [Buf("vs%d" % i) for i in range(2)]
    vt = P.sb("vt", [128, 3, 32, 128], BF16)
    b_vt = [Buf("vt%d" % i) for i in range(3)]
    oacc = P.sb("oacc", [128, 2, NOWN], F32)
    b_oacc = Buf("oacc")
    mdil = P.sb("mdil", [128, 3 * 384 + 256], BF16)
    b_mdil = Buf("mdil")
    natab = P.sb("natab", [128, 5 * 896], BF16)
    b_natab = Buf("natab")
    et = [P.sb("et", [128, 896], BF16) for _ in range(3)]
    b_et = [Buf("et%d" % i) for i in range(3)]
    sq = P.sb("sq", [128, NOWN], BF16)
    b_sq = Buf("sq")
    rs_t = P.sb("rs_t", [128, NOWN], F32)
    b_rs = Buf("rs_t")
    ob = [P.sb("ob", [128, NOWN], BF16) for _ in range(2)]
    b_ob = [Buf("ob%d" % i) for i in range(2)]
    outg = P.sb("outg", [128, 16], F32)
    b_outg = Buf("outg")
    P.dma("sp", lambda e: e.dma_start(out=outg[:], in_=outg_d[:, :]), b_outg, writes=[b_outg])
    P.dma("pool", lambda e: e.dma_start(out=mdil[:], in_=mdil_d[:, :]), b_mdil, writes=[b_mdil])
    SCALE = float(128 ** -0.5)
    n_et = 0
    n_sb = 0
    n_yb = 0

    def load_head(h):
        s = h % 2
        P.dma("sp", lambda e: e.dma_start(out=qs_[s][:], in_=qT_d[h]), b_qs[s], reads=[b_qT[h]], writes=[b_qs[s]])
        P.dma("sp", lambda e: e.dma_start(out=ks_[s][:], in_=kT_d[h]), b_ks[s], reads=[b_kT[h]], writes=[b_ks[s]])
        P.dma("sp", lambda e: e.dma_start(out=vs_[s][:], in_=vT_d[h]), b_vs[s], reads=[b_vT[h]], writes=[b_vs[s]])

    def make_vt(s, fold, d, tile_list):
        per_res = 32 // d
        for g0 in range(0, len(tile_list), 8):
            grp = tile_list[g0:g0 + 8]
            bank = 6 + (g0 // 8) % 2
            pT = psb(bank, 1)
            for n, ti in enumerate(grp):
                r, mt = ti // per_res, ti % per_res
                start = r + d * 128 * mt
                src = vs_[s][:, start:start + d * 127 + 1:d]
                P.op("pe", lambda e, n=n, src=src, pT=pT: e.transpose(out=pT[:, n * 128:(n + 1) * 128], in_=src,
                                                                      identity=ident_b[:]),
                     reads=[b_vs[s], b_const], writes=[b_ps[bank]], inc=(n == len(grp) - 1))
            assert grp == list(range(grp[0], grp[0] + len(grp)))
            dst = vt[:, fold, grp[0]:grp[0] + len(grp), :]
            srcv = pT[:, 0:len(grp) * 128].rearrange("p (a b) -> p a b", b=128)
            if (g0 // 8) % 2 == 0:
                P.op("act", lambda e, dst=dst, srcv=srcv: e.copy(out=dst, in_=srcv), reads=[b_ps[bank]],
                     writes=[b_vt[fold]])
            else:
                P.op("dve", lambda e, dst=dst, srcv=srcv: e.tensor_copy(out=dst, in_=srcv), reads=[b_ps[bank]],
                     writes=[b_vt[fold]])

    def attn_group(s, q_ap, key_aps, mask_aps, vt_aps, mask_buf, fold, o_ap, first, esc=None):
        nonlocal n_et, n_sb, n_yb
        nt = len(key_aps)
        esc = SCALE if esc is None else esc
        sb0 = 2 * (n_sb % 2)
        n_sb += 1
        es = n_et % 3
        n_et += 1
        yb = 4 + n_yb % 2
        n_yb += 1
        sflat = ps[:, sb0:sb0 + 2, :].rearrange("p a b -> p (a b)")
        for j in range(nt):
            reg = sflat[:, j * 128:(j + 1) * 128]
            bank = sb0 + (j * 128) // 512
            P.op("pe", lambda e, reg=reg, m=mask_aps[j]: e.matmul(reg, lhsT=ident_b[:], rhs=m, start=True, stop=False),
                 reads=[b_const, mask_buf], writes=[b_ps[bank]], inc=False)
            P.op("pe", lambda e, reg=reg, k=key_aps[j]: e.matmul(reg, lhsT=k, rhs=q_ap, start=False, stop=True),
                 reads=[b_ks[s], b_qs[s]], writes=[b_ps[bank]], inc=(j == nt - 1 or j == 3))
        w0 = min(nt, 4) * 128
        P.op("act", lambda e: e.activation(out=et[es][:, 0:w0], in_=sflat[:, 0:w0], func=AF.Exp, scale=esc),
             reads=[b_ps[sb0]], writes=[b_et[es]])
        if nt > 4:
            P.op("act", lambda e: e.activation(out=et[es][:, w0:nt * 128], in_=sflat[:, w0:nt * 128], func=AF.Exp,
                                               scale=esc),
                 reads=[b_ps[sb0 + 1]], writes=[b_et[es]])
        def pv_part():
            for j in range(nt):
                P.op("pe", lambda e, j=j: e.matmul(ps[:, yb, 0:128], lhsT=vt_aps[j],
                                                   rhs=et[es][:, j * 128:(j + 1) * 128],
                                                   start=(j == 0), stop=(j == nt - 1)),
                     reads=[b_vt[fold], b_et[es]], writes=[b_ps[yb]], inc=False)
            for j in range(nt):
                P.op("pe", lambda e, j=j: e.matmul(ps[:, yb, 128:256], lhsT=ones_b[:],
                                                   rhs=et[es][:, j * 128:(j + 1) * 128],
                                                   start=(j == 0), stop=(j == nt - 1)),
                     reads=[b_const, b_et[es]], writes=[b_ps[yb]], inc=(j == nt - 1))
            yv = ps[:, yb, 0:256].rearrange("p (a b) -> p a b", a=2)
            if first:
                P.op("dve", lambda e: e.tensor_copy(out=o_ap, in_=yv), reads=[b_ps[yb]], writes=[b_oacc])
            else:
                P.op("dve", lambda e: e.tensor_tensor(out=o_ap, in0=yv, in1=o_ap, op=ALU.add),
                     reads=[b_ps[yb], b_oacc], writes=[b_oacc])
        pending.append(pv_part)
        if len(pending) > 1:
            pending.pop(0)()

    pending = []

    def attn_flush():
        while pending:
            pending.pop(0)()

    load_head(0)
    for h in range(16):
        s = h % 2
        if h + 1 < 16:
            load_head(h + 1)
        if h < 4:
            P.dma("pool", lambda e, h=h: e.dma_start(out=natab[:], in_=natab_d[h]), b_natab, writes=[b_natab])
            make_vt(s, 0, 1, list(range(5, 27)))
            for j in range(16):
                var = 0 if j == 0 else 1 if j == 1 else 3 if j == 14 else 4 if j == 15 else 2
                q_ap = qs_[s][:, j * 128:(j + 1) * 128]
                kt0 = 5 + j
                key_aps = [ks_[s][:, (kt0 + n) * 128:(kt0 + n + 1) * 128] for n in range(7)]
                mask_aps = [natab[:, var * 896 + n * 128: var * 896 + (n + 1) * 128] for n in range(7)]
                vt_aps = [vt[:, 0, kt0 + n, :] for n in range(7)]
                o_ap = oacc[:, :, j * 128:(j + 1) * 128]
                attn_group(s, q_ap, key_aps, mask_aps, vt_aps, b_natab, 0, o_ap, True, esc=1.0)
        else:
            make_vt(s, 0, 1, list(range(7, 25)))
            for r in range(4):
                make_vt(s, 1, 4, list(range(r * 8 + 1, r * 8 + 7)))
            for r0 in range(0, 16, 4):
                make_vt(s, 2, 16, list(range(r0 * 2, r0 * 2 + 8)))
            for mt in range(8, 24):
                var = 0 if mt == 8 else 2 if mt == 23 else 1
                q_ap = qs_[s][:, (mt - 8) * 128:(mt - 7) * 128]
                key_aps = [ks_[s][:, (mt - 1 + n) * 128:(mt + n) * 128] for n in range(3)]
                mask_aps = [mdil[:, var * 384 + n * 128: var * 384 + (n + 1) * 128] for n in range(3)]
                vt_aps = [vt[:, 0, mt - 1 + n, :] for n in range(3)]
                o_ap = oacc[:, :, (mt - 8) * 128:(mt - 7) * 128]
                attn_group(s, q_ap, key_aps, mask_aps, vt_aps, b_mdil, 0, o_ap, True)
            for r in range(4):
                for mt in range(2, 6):
                    var = 0 if mt == 2 else 2 if mt == 5 else 1
                    q0 = r + 4 * 128 * mt - OWN0
                    q_ap = qs_[s][:, q0:q0 + 4 * 127 + 1:4]
                    key_aps = []
                    for n in range(3):
                        k0 = r + 4 * 128 * (mt - 1 + n)
                        key_aps.append(ks_[s][:, k0:k0 + 4 * 127 + 1:4])
                    mask_aps = [mdil[:, var * 384 + n * 128: var * 384 + (n + 1) * 128] for n in range(3)]
                    vt_aps = [vt[:, 1, r * 8 + mt - 1 + n, :] for n in range(3)]
                    o_ap = oacc[:, :, q0:q0 + 4 * 127 + 1:4]
                    attn_group(s, q_ap, key_aps, mask_aps, vt_aps, b_mdil, 1, o_ap, False)
            for r in range(16):
                q0 = r + 16 * 64 - OWN0
                q_ap = qs_[s][:, q0:q0 + 16 * 127 + 1:16]
                key_aps = []
                for n in range(2):
                    k0 = r + 16 * 128 * n
                    key_aps.append(ks_[s][:, k0:k0 + 16 * 127 + 1:16])
                mask_aps = [mdil[:, 1152 + n * 128: 1152 + (n + 1) * 128] for n in range(2)]
                vt_aps = [vt[:, 2, r * 2 + n, :] for n in range(2)]
                o_ap = oacc[:, :, q0:q0 + 16 * 127 + 1:16]
                attn_group(s, q_ap, key_aps, mask_aps, vt_aps, b_mdil, 2, o_ap, False)
        attn_flush()
        P.op("dve", lambda e: e.reciprocal(out=oacc[:, 1, :], in_=oacc[:, 1, :]), reads=[b_oacc], writes=[b_oacc])
        P.op("dve", lambda e: e.tensor_tensor(out=oacc[:, 0, :], in0=oacc[:, 0, :], in1=oacc[:, 1, :], op=ALU.mult),
             reads=[b_oacc], writes=[b_oacc])
        P.op("act", lambda e: e.activation(out=sq[:], in_=oacc[:, 0, :], func=AF.Square), reads=[b_oacc],
             writes=[b_sq])
        obs = h % 2
        for c4 in range(4):
            bank = 6 + c4 % 2
            P.op("pe", lambda e, c4=c4, bank=bank: e.matmul(ps[:, bank, :], lhsT=ones_b[:],
                                                            rhs=sq[:, c4 * 512:(c4 + 1) * 512], start=True, stop=True),
                 reads=[b_sq, b_const], writes=[b_ps[bank]])
            P.op("act", lambda e, c4=c4, bank=bank: e.activation(out=rs_t[:, c4 * 512:(c4 + 1) * 512],
                                                                 in_=ps[:, bank, :], func=AF.Sqrt,
                                                                 bias=eps_t[:, 0:1], scale=1.0 / 128),
                 reads=[b_ps[bank], b_eps], writes=[b_rs])
        P.op("dve", lambda e: e.reciprocal(out=rs_t[:], in_=rs_t[:]), reads=[b_rs], writes=[b_rs])
        P.op("dve", lambda e, h=h, obs=obs: e.scalar_tensor_tensor(out=ob[obs][:], in0=oacc[:, 0, :],
                                                                   scalar=outg[:, h:h + 1], in1=rs_t[:],
                                                                   op0=ALU.mult, op1=ALU.mult),
             reads=[b_oacc, b_outg, b_rs], writes=[b_ob[obs]])
        P.dma("sp", lambda e, h=h, obs=obs: e.dma_start(out=oT_d[h], in_=ob[obs][:]), b_ob[obs],
              reads=[b_ob[obs]], writes=[b_oT[h]])
    if stop_phase <= 3:
        return finish(P, nc, b_oT + b_h1)
    P.barrier()
    P.sb_reset(arena0)

    B = alloc_ffn_bufs()
    pin = [P.sb("pin", [128, 256], F32) for _ in range(2)]
    b_pin = [Buf("pin%d" % i) for i in range(2)]
    pnb = P.sb("pnb", [128, 256], BF16)
    b_pnb = Buf("pnb")
    pTt = P.sb("pTt", [128, 2, T], BF16)
    b_pTt = [Buf("pTt%d" % i) for i in range(4)]
    wpp_s = [P.sb("wpp_s", [128, 2, 512], BF16) for _ in range(2)]
    b_wpp = [Buf("wpp%d" % i) for i in range(2)]
    n_wpp = 0
    for tt in range(NOWN // T):
        P.dma("sp", lambda e, tt=tt: e.dma_start(out=B.uT[:], in_=oT_d[:, :, tt * T:(tt + 1) * T].rearrange(
            "h p t -> p h t")), B.b_uT[0], reads=b_oT, writes=B.b_uT)
        load_gain(B, "post", 3)

        def evac_o(i, cb):
            if (i + cb) % 2 == 0:
                P.op("act", lambda e: e.copy(out=B.ft[i][:, cb * 512:(cb + 1) * 512], in_=ps[:, 4 + i, :]),
                     reads=[b_ps[4 + i]], writes=[B.b_ft[i]])
            else:
                P.op("dve", lambda e: e.tensor_copy(out=B.ft[i][:, cb * 512:(cb + 1) * 512], in_=ps[:, 4 + i, :]),
                     reads=[b_ps[4 + i]], writes=[B.b_ft[i]])
        gemm_mov(B, wo_d, 1, 16, lambda k, i: B.uT[:, k, i * 128:(i + 1) * 128], lambda k, i: B.b_uT, evac_o,
                 w_c=wo_c, wkey=("o",), first=(tt == 0))
        for i in range(4):
            oi = tt * 4 + i
            post_residual(B, i, h1_d[oi * 128:(oi + 1) * 128, :], b_h1[oi], 1.0, si=i % 2)
            P.dma("sp", lambda e, i=i, oi=oi: e.dma_start(out=h2_d[oi * 128:(oi + 1) * 128, :], in_=B.ft[i][:]),
                  B.b_ft[i], reads=[B.b_ft[i]], writes=[b_h2[oi]])
        ffn_stage(B, 1, lambda i, tt=tt: h2_d[(tt * 4 + i) * 128:(tt * 4 + i + 1) * 128, :],
                  lambda i, tt=tt: b_h2[tt * 4 + i], 4, first=(tt == 0))
        for i in range(4):
            oi = tt * 4 + i
            P.dma("sp", lambda e, i=i, oi=oi: e.dma_start(out=h3_d[oi * 128:(oi + 1) * 128, :], in_=B.ft[i][:]),
                  B.b_ft[i], reads=[B.b_ft[i]], writes=[b_h3[oi]])
        load_gain(B, "pre", 6)
        load_gain(B, "post", 7)
        for i in range(4):
            oi = tt * 4 + i
            norm_transpose(B, B.ft[i], B.b_ft[i], i, si=i % 2)
            pi = i % 2
            P.dma("sp", lambda e, pi=pi, oi=oi: e.dma_start(out=pin[pi][:], in_=p_d[oi * 128:(oi + 1) * 128, :]),
                  b_pin[pi], writes=[b_pin[pi]])
            P.op("dve", lambda e, pi=pi: e.tensor_copy(out=pnb[:], in_=pin[pi][:]), reads=[b_pin[pi]], writes=[b_pnb])
            pT = psb(2, 1)
            for c in range(2):
                P.op("pe", lambda e, c=c, pT=pT: e.transpose(out=pT[:, c * 128:(c + 1) * 128],
                                                             in_=pnb[:, c * 128:(c + 1) * 128], identity=ident_b[:]),
                     reads=[b_pnb, b_const], writes=[b_ps[2]], inc=(c == 1))
            P.op("act", lambda e, i=i, pT=pT: e.copy(out=pTt[:, :, i * 128:(i + 1) * 128],
                                                     in_=pT[:, 0:256].rearrange("p (c t) -> p c t", c=2)),
                 reads=[b_ps[2]], writes=[b_pTt[i]])

        def evac_p(i, cb):
            nonlocal n_wpp
            ws = cb % 2
            pb = i % 2
            for c in range(2):
                P.op("pe", lambda e, c=c: e.matmul(ps[:, pb, :], lhsT=pTt[:, c, i * 128:(i + 1) * 128],
                                                   rhs=wpp_s[ws][:, c, :], start=(c == 0), stop=(c == 1)),
                     reads=[b_pTt[i], b_wpp[ws]], writes=[b_ps[pb]], inc=(c == 1))
            sgs = i % 2
            P.op("act", lambda e: e.activation(out=B.sg[sgs][:], in_=ps[:, 4 + i, :], func=AF.Sigmoid),
                 reads=[b_ps[4 + i]], writes=[B.b_sg[sgs]])
            P.op("dve", lambda e: e.tensor_tensor(out=B.ft[i][:, cb * 512:(cb + 1) * 512], in0=ps[:, pb, :],
                                                  in1=B.sg[sgs][:], op=ALU.mult),
                 reads=[b_ps[pb], B.b_sg[sgs]], writes=[B.b_ft[i]])

        def gemm_ple():
            for cb in range(4):
                ws = cb % 2
                wload(wpp_s[ws][:], b_wpp[ws], wpp_d[cb, 0], wpp_c[cb, 0], ("pp", cb), tt == 0)
                s = B.wmv_n % 2
                B.wmv_n += 1
                wload(B.wmv[s][:], B.b_wmv[s], wpg_d[cb, 0], wpg_c[cb, 0], ("pg", cb), tt == 0)
                for i in range(4):
                    for k in range(16):
                        P.op("pe", lambda e, s=s, i=i, k=k: e.matmul(
                            ps[:, 4 + i, :], lhsT=B.uT[:, k, i * 128:(i + 1) * 128], rhs=B.wmv[s][:, k, :],
                            start=(k == 0), stop=(k == 15)),
                            reads=[B.b_wmv[s], B.b_uT[i]], writes=[b_ps[4 + i]], inc=(k == 15))
                for i in range(4):
                    evac_p(i, cb)
        gemm_ple()
        for i in range(4):
            oi = tt * 4 + i
            post_residual(B, i, h3_d[oi * 128:(oi + 1) * 128, :], b_h3[oi], 1.0, si=i % 2)
            P.dma("sp", lambda e, i=i, oi=oi: e.dma_start(out=out_d[oi * 128:(oi + 1) * 128, :], in_=B.ft[i][:]),
                  B.b_ft[i], reads=[B.b_ft[i]], writes=[b_out[oi]])
    return finish(P, nc, b_out)


def finish(P, nc, bufs):
    P.barrier()
    P.emit()
    P.close()
    return nc


def lay_stat(W):
    K, Fo = W.shape
    return np.ascontiguousarray(W.reshape(K // 128, 128, Fo // 128, 128).transpose(2, 1, 0, 3))


def lay_mov(W, kg):
    K, Do = W.shape
    nj = K // 128
    return np.ascontiguousarray(W.reshape(nj // kg, kg, 128, Do // 512, 512).transpose(3, 0, 2, 1, 4))


def dil_masks(q):
    k = np.arange(128)[:, None]
    qq = np.arange(128)[None, :]
    tabs = []
    for var in range(3):
        t = np.zeros((128, 384), np.float32)
        for j in range(3):
            ok = np.abs(128 * (j - 1) + k - qq) <= 64
            if var == 0 and j == 0 and q == 0:
                ok = np.zeros_like(ok)
            if var == 2 and j == 2 and q == 3:
                ok = np.zeros_like(ok)
            t[:, j * 128:(j + 1) * 128] = np.where(ok, 0.0, MASKV)
        tabs.append(t)
    t = np.zeros((128, 256), np.float32)
    for j in range(2):
        mk = 128 * j + k
        mq = 64 + qq
        ok = np.abs(mk - mq) <= 64
        if q == 0:
            ok = ok & (mk >= 64)
        if q == 3:
            ok = ok & (mk < 192)
        t[:, j * 128:(j + 1) * 128] = np.where(ok, 0.0, MASKV)
    tabs.append(t)
    return np.ascontiguousarray(np.concatenate(tabs, axis=1))


def na_tables(q, rpb):
    out = np.full((4, 128, 5, 7, 128), MASKV, np.float32)
    jsel = [0, 1, 7, 14, 15]
    kk = np.arange(128)
    k_ro, k_c = kk // 64, kk % 64
    qq = np.arange(128)
    q_ro, q_c = qq // 64, qq % 64
    for vi, j in enumerate(jsel):
        r0 = 32 * q + 2 * j
        rg = r0 + q_ro
        rs = np.clip(rg - 4, 0, 128 - 8)
        qs = np.clip(q_c - 8, 0, 64 - 16)
        for n in range(7):
            kr = r0 - 6 + 2 * n + k_ro
            ok = (kr[:, None] >= rs[None, :]) & (kr[:, None] < rs[None, :] + 8) & \
                 (k_c[:, None] >= qs[None, :]) & (k_c[:, None] < qs[None, :] + 16) & \
                 (kr[:, None] >= 0) & (kr[:, None] < 128)
            dr = np.clip(kr[:, None] - rg[None, :] + 7, 0, 14)
            dc = np.clip(k_c[:, None] - q_c[None, :] + 15, 0, 30)
            for h in range(4):
                out[h, :, vi, n, :] = np.where(ok, rpb[h][dr, dc], MASKV)
    return np.ascontiguousarray(out.reshape(4, 128, 5 * 896))


def rope_tables(q):
    g0 = q * NOWN - OWN0
    inv = np.float32(10000.0) ** (-np.arange(0, 128, 2, dtype=np.float32) / np.float32(128))
    s = (g0 + np.arange(NTOK)).astype(np.float32)
    ang = (s[:, None] * inv[None, :]).astype(np.float32)
    c = np.cos(ang).astype(np.float32).T
    sn = np.sin(ang).astype(np.float32).T
    return (np.ascontiguousarray(np.concatenate([c, c], 0)), np.ascontiguousarray(np.concatenate([sn, sn], 0)))


def make_consts():
    ident = np.eye(128, dtype=np.float32)
    rot = np.zeros((128, 128), np.float32)
    for m in range(64):
        rot[m + 64, m] = -1.0
        rot[m, m + 64] = 1.0
    ones = np.ones((128, 128), np.float32)
    return np.stack([ident, rot, ones])


def prep_inputs(inputs, cores=range(8)):
    f = lambda a: np.asarray(a, dtype=np.float32)
    x = f(inputs["x"])
    p = f(inputs["p"])[0]
    shared = {
        "wg1": lay_stat(f(inputs["ffn1_w_gate"])[0]), "wu1": lay_stat(f(inputs["ffn1_w_up"])[0]),
        "wd1": lay_mov(f(inputs["ffn1_w_down"])[0], 11),
        "wg2": lay_stat(f(inputs["ffn2_w_gate"])[0]), "wu2": lay_stat(f(inputs["ffn2_w_up"])[0]),
        "wd2": lay_mov(f(inputs["ffn2_w_down"])[0], 11),
        "wqkv": lay_stat(f(inputs["w_qkv"])[0]),
        "wo": lay_mov(f(inputs["w_o"])[0], 16),
        "wpg": lay_mov(f(inputs["w_ple_gate"])[0], 16),
        "wpp": lay_mov(f(inputs["w_ple_proj"])[0], 2),
        "gains": np.ascontiguousarray(np.stack([f(inputs[k])[0] for k in (
            "ffn1_pre_g", "ffn1_post_g", "mix_pre_g", "mix_post_g", "ffn2_pre_g", "ffn2_post_g",
            "ple_pre_g", "ple_post_g")])),
        "outg": np.ascontiguousarray(f(inputs["out_g"])[0].reshape(16, 128).T),
        "consts": make_consts(),
    }
    rpb = f(inputs["na_rpb"])[0]
    in_maps = []
    for c in cores:
        b, q = c // 4, c % 4
        g0 = q * NOWN - OWN0
        xh = np.zeros((NTOK, D), np.float32)
        lo, hi = max(g0, 0), min(g0 + NTOK, SEQ)
        xh[lo - g0:hi - g0] = x[b, lo:hi]
        cs, sn = rope_tables(q)
        m = dict(shared)
        m.update({"x": xh, "p": np.ascontiguousarray(p[b, q * NOWN:(q + 1) * NOWN]),
                  "cos": cs, "sin": sn, "mdil": dil_masks(q), "natab": na_tables(q, rpb)})
        in_maps.append(m)
    return in_maps


_NC_CACHE = {}


def kernel(**inputs):
    in_maps = prep_inputs(inputs)
    if "nc" not in _NC_CACHE:
        _NC_CACHE["nc"] = build_program()
    res = run_bass_kernel_spmd(_NC_CACHE["nc"], in_maps, core_ids=list(range(8)))
    out = np.zeros((2, SEQ, D), np.float32)
    for c in range(8):
        b, q = c // 4, c % 4
        out[b, q * NOWN:(q + 1) * NOWN] = res.results[c]["out"]
    return out
```

```python
import numpy as np
import concourse.bass as bass
import concourse.mybir as mybir
from concourse.bass_utils import run_bass_kernel_spmd

F32 = mybir.dt.float32
BF16 = mybir.dt.bfloat16
AF = mybir.ActivationFunctionType
ALU = mybir.AluOpType

ENGS = ("pe", "act", "dve", "pool", "sp")

D = 2048
DFF = 5632
NJ = DFF // 128
SEQ = 8192
NTOK = 4096
OWN0 = 1024
NOWN = 2048
T = 512
EPS = 1e-6
MASKV = -30000.0
SB_BASE = 16512
SB_CAP = 212800


class Buf:
    __slots__ = ("name", "w", "r", "sem", "semcnt")

    def __init__(self, name):
        self.name = name
        self.w = None
        self.r = {}
        self.sem = None
        self.semcnt = 0


class Prog:
    def __init__(self, nc):
        self.nc = nc
        self.ops = {e: [] for e in ENGS}
        self.cnt = {e: 0 for e in ENGS}
        self.sems = {}
        self.seen = {e: {} for e in ENGS}
        self._stack = []
        self.dma_bufs = []
        self.sb_off = SB_BASE
        self.nalloc = 0

    def new_sem(self, name):
        self.nsem = getattr(self, "nsem", 0) + 1
        cm = self.nc.semaphore("%s_%d" % (name, self.nsem))
        s = cm.__enter__()
        self._stack.append(cm)
        return s

    def eng_sem(self, e):
        if e not in self.sems:
            self.sems[e] = self.new_sem("s_" + e)
        return self.sems[e]

    def sb(self, name, shape, dtype):
        nbytes = int(np.prod(shape[1:])) * (4 if dtype == F32 else 2)
        nbytes = (nbytes + 63) // 64 * 64
        off = self.sb_off
        assert off + nbytes <= SB_BASE + SB_CAP, ("SBUF overflow", name, off + nbytes - SB_BASE)
        self.sb_off += nbytes
        self.nalloc += 1
        return self.nc.alloc_sbuf_tensor_at("%s_%d" % (name, self.nalloc), list(shape), dtype, offset=off)

    def sb_mark(self):
        return self.sb_off

    def sb_reset(self, mark):
        self.sb_off = mark

    def _deps(self, eng, reads, writes):
        need = {}

        def add(ev):
            if ev is None:
                return
            k, c = ev
            if need.get(k, 0) < c:
                need[k] = c
        for b in reads:
            add(b.w)
        for b in writes:
            add(b.w)
            for k, c in b.r.items():
                add((k, c))
        waits = []
        seen = self.seen[eng]
        for k, c in need.items():
            if k == "pe" and eng == "pe":
                continue
            if seen.get(k, 0) >= c:
                continue
            seen[k] = c
            waits.append((k, c))
        return waits

    def _sem_of(self, k):
        return self.eng_sem(k) if isinstance(k, str) else k

    def op(self, eng, fn, reads=(), writes=(), inc=True):
        waits = [(self._sem_of(k), c) for k, c in self._deps(eng, reads, writes)]
        if inc:
            self.cnt[eng] += 1
            ev = (eng, self.cnt[eng])
            sem = self.eng_sem(eng)
        else:
            ev = (eng, self.cnt[eng] + 1)
            sem = None

        def run(e, waits=waits, fn=fn, sem=sem):
            for s, c in waits:
                e.wait_ge(s, c)
            ins = fn(e)
            if sem is not None:
                ins.then_inc(sem, 1)
        self.ops[eng].append(run)
        for b in reads:
            if b.r.get(ev[0], 0) < ev[1]:
                b.r[ev[0]] = ev[1]
        for b in writes:
            b.w = ev
            b.r = {}
        return ev

    def dma(self, q, fn, owner, reads=(), writes=()):
        waits = [(self._sem_of(k), c) for k, c in self._deps(q, reads, writes)]
        if owner.sem is None:
            owner.sem = self.new_sem("d_" + owner.name)
            self.dma_bufs.append(owner)
        owner.semcnt += 16
        sem, c = owner.sem, owner.semcnt

        def run(e, waits=waits, fn=fn, sem=sem):
            for s, cc in waits:
                e.wait_ge(s, cc)
            fn(e).then_inc(sem, 16)
        self.ops[q].append(run)
        ev = (sem, c)
        for b in reads:
            if b.r.get(sem, 0) < c:
                b.r[sem] = c
        for b in writes:
            b.w = ev
            b.r = {}
        return ev

    def barrier(self):
        evs = [(e, self.cnt[e]) for e in ENGS if self.cnt[e] > 0]
        evs += [(b.sem, b.semcnt) for b in self.dma_bufs if b.semcnt > 0]
        for eng in ENGS:
            waits = []
            for k, c in evs:
                if k == eng:
                    continue
                if self.seen[eng].get(k, 0) >= c:
                    continue
                self.seen[eng][k] = c
                waits.append((self._sem_of(k), c))

            def run(e, waits=waits):
                for s, c in waits:
                    e.wait_ge(s, c)
            self.ops[eng].append(run)

    def emit(self):
        nc = self.nc
        with nc.Block() as block:
            @block.tensor
            def _(e):
                for f in self.ops["pe"]:
                    f(e)

            @block.scalar
            def _(e):
                for f in self.ops["act"]:
                    f(e)

            @block.vector
            def _(e):
                for f in self.ops["dve"]:
                    f(e)

            @block.gpsimd
            def _(e):
                for f in self.ops["pool"]:
                    f(e)

            @block.sync
            def _(e):
                for f in self.ops["sp"]:
                    f(e)

    def close(self):
        while self._stack:
            self._stack.pop().__exit__(None, None, None)


def build_program(stop_phase=4, dbg=False):
    nc = bass.Bass("TRN2", target_bir_lowering=False)
    P = Prog(nc)
    r0 = nc.bump_sbuf(SB_CAP)
    assert r0 is not None and r0[0] == SB_BASE, r0

    def din(name, shape, dt=F32):
        return nc.dram_tensor(name, list(shape), dt, kind="ExternalInput").ap()

    def dscr(name, shape, dt):
        kind = "ExternalOutput" if dbg else "Internal"
        return nc.dram_tensor(name, list(shape), dt, kind=kind).ap()

    x_d = din("x", [NTOK, D])
    p_d = din("p", [NOWN, 256])
    wg_d = [din("wg1", [NJ, 128, 16, 128]), din("wg2", [NJ, 128, 16, 128])]
    wu_d = [din("wu1", [NJ, 128, 16, 128]), din("wu2", [NJ, 128, 16, 128])]
    wd_d = [din("wd1", [4, 4, 128, 11, 512]), din("wd2", [4, 4, 128, 11, 512])]
    wqkv_d = din("wqkv", [48, 128, 16, 128])
    wo_d = din("wo", [4, 1, 128, 16, 512])
    wpg_d = din("wpg", [4, 1, 128, 16, 512])
    wpp_d = din("wpp", [4, 1, 128, 2, 512])
    gains_d = din("gains", [8, D])
    outg_d = din("outg", [128, 16])
    gainsT_d = din("gainsT", [128, 64])
    consts_d = din("consts", [3, 128, 128])
    cos_d = din("cos", [128, NTOK])
    sin_d = din("sin", [128, NTOK])
    mdil_d = din("mdil", [128, 3 * 384 + 256])
    natab_d = din("natab", [4, 128, 5 * 896])
    out_d = nc.dram_tensor("out", [NOWN, D], F32, kind="ExternalOutput").ap()

    h1_d = dscr("h1s", [NOWN, D], F32)
    h2_d = dscr("h2s", [NOWN, D], F32)
    h3_d = dscr("h3s", [NOWN, D], F32)
    u2T_d = dscr("u2Ts", [16, 128, NTOK], BF16)
    qT_d = dscr("qTs", [16, 128, NOWN], BF16)
    kT_d = dscr("kTs", [16, 128, NTOK], BF16)
    vT_d = dscr("vTs", [16, 128, NTOK], BF16)
    oT_d = dscr("oTs", [16, 128, NOWN], BF16)

    def dint(name, shape, dt):
        return nc.dram_tensor(name, list(shape), dt, kind="Internal").ap()
    wg_c = [dint("wg1c", [NJ, 128, 16, 128], BF16), dint("wg2c", [NJ, 128, 16, 128], BF16)]
    wu_c = [dint("wu1c", [NJ, 128, 16, 128], BF16), dint("wu2c", [NJ, 128, 16, 128], BF16)]
    wd_c = [dint("wd1c", [4, 4, 128, 11, 512], BF16), dint("wd2c", [4, 4, 128, 11, 512], BF16)]
    wo_c = dint("woc", [4, 1, 128, 16, 512], BF16)
    wpg_c = dint("wpgc", [4, 1, 128, 16, 512], BF16)
    wpp_c = dint("wppc", [4, 1, 128, 2, 512], BF16)
    cache_bufs = {}

    def cbuf(key):
        if key not in cache_bufs:
            cache_bufs[key] = Buf("wc%d" % len(cache_bufs))
        return cache_bufs[key]

    def wload(dst_ap, slot_buf, w32_ap, wc_ap, key, first):
        if first:
            P.dma("pool", lambda e: e.dma_start(out=dst_ap, in_=w32_ap), slot_buf, writes=[slot_buf])
            P.dma("sp", lambda e: e.dma_start(out=wc_ap, in_=dst_ap), slot_buf, reads=[slot_buf], writes=[cbuf(key)])
        else:
            P.dma("pool", lambda e: e.dma_start(out=dst_ap, in_=wc_ap), slot_buf, reads=[cbuf(key)],
                  writes=[slot_buf])

    b_h1 = [Buf("h1_%d" % i) for i in range(16)]
    b_h2 = [Buf("h2_%d" % i) for i in range(16)]
    b_h3 = [Buf("h3_%d" % i) for i in range(16)]
    b_out = [Buf("out_%d" % i) for i in range(16)]
    b_u2T = [Buf("u2T_%d" % i) for i in range(8)]
    b_qT = [Buf("qT_%d" % i) for i in range(16)]
    b_kT = [Buf("kT_%d" % i) for i in range(16)]
    b_vT = [Buf("vT_%d" % i) for i in range(16)]
    b_oT = [Buf("oT_%d" % i) for i in range(16)]

    ps = nc.alloc_psum_tensor("ps", [128, 8, 512], F32)
    b_ps = [Buf("ps%d" % i) for i in range(8)]

    def psb(bank, n=1):
        return ps[:, bank:bank + n, :].bitcast(BF16).rearrange("p a b -> p (a b)")

    ident_b = P.sb("ident_b", [128, 128], BF16)
    rot_b = P.sb("rot_b", [128, 128], BF16)
    ones_b = P.sb("ones_b", [128, 128], BF16)
    eps_t = P.sb("eps_t", [128, 1], F32)
    stat = P.sb("stat", [128, 16], F32)
    b_const = Buf("const")
    b_eps = Buf("eps")
    P.dma("pool", lambda e: e.dma_start(out=ident_b[:], in_=consts_d[0]), b_const, writes=[b_const])
    P.dma("pool", lambda e: e.dma_start(out=rot_b[:], in_=consts_d[1]), b_const, writes=[b_const])
    P.dma("pool", lambda e: e.dma_start(out=ones_b[:], in_=consts_d[2]), b_const, writes=[b_const])
    P.op("dve", lambda e: e.memset(eps_t[:], EPS), writes=[b_eps])
    b_stat = [Buf("stat%d" % i) for i in range(4)]
    arena0 = P.sb_mark()

    class FFNBufs:
        pass

    def alloc_ffn_bufs(pipelined=False):
        B = FFNBufs()
        B.xin = [P.sb("xin", [128, D], F32) for _ in range(2)]
        B.b_xin = [Buf("xin%d" % i) for i in range(2)]
        B.ft = [P.sb("ft", [128, D], F32) for _ in range(4)]
        B.b_ft = [Buf("ft%d" % i) for i in range(4)]
        B.xn = P.sb("xn", [128, D], BF16)
        B.b_xn = Buf("xn")
        B.junk = P.sb("junk", [128, D], BF16)
        B.b_junk = Buf("junk")
        nu = 2 if pipelined else 1
        B.uTs = [P.sb("uT", [128, 16, T], BF16) for _ in range(nu)]
        B.b_uTs = [[Buf("uT%d_%d" % (u, i)) for i in range(4)] for u in range(nu)]
        B.uT, B.b_uT = B.uTs[0], B.b_uTs[0]
        B.aT = P.sb("aT", [128, NJ, T], BF16)
        B.b_aT = [Buf("aT%d" % i) for i in range(NJ)]
        B.wgu = [P.sb("wgu", [128, 2, 16, 128], BF16) for _ in range(3)]
        B.b_wgu = [Buf("wgu%d" % i) for i in range(3)]
        B.wmv = [P.sb("wmv", [128, 11 if pipelined else 16, 512], BF16) for _ in range(2)]
        B.b_wmv = [Buf("wmv%d" % i) for i in range(2)]
        if pipelined:
            B.gT = P.sb("gT", [128, 64], F32)
            B.b_gT = Buf("gT")
        else:
            B.gpre = P.sb("gpre", [128, D], F32)
            B.b_gpre = Buf("gpre")
        B.gpost = P.sb("gpost", [128, D], F32)
        B.b_gpost = Buf("gpost")
        B.sg = [P.sb("sg", [128, 512], F32) for _ in range(2)]
        B.b_sg = [Buf("sg%d" % i) for i in range(2)]
        B.wmv_n = 0
        B.wgu_n = 0
        return B

    def load_gain(B, which, idx):
        t, b = (B.gpre, B.b_gpre) if which == "pre" else (B.gpost, B.b_gpost)
        P.dma("sp", lambda e: e.dma_start(out=t[:], in_=gains_d[idx:idx + 1, :].partition_broadcast(128)),
              b, writes=[b])

    def rms_rstd(B, src, b_src, si):
        st = stat[:, si:si + 1]
        P.op("act", lambda e: e.activation(out=B.junk[:], in_=src[:], func=AF.Square,
                                           scale=float(D ** -0.5), accum_out=st),
             reads=[b_src], writes=[B.b_junk, b_stat[si]])
        P.op("act", lambda e: e.activation(out=st, in_=st, func=AF.Sqrt, bias=eps_t[:, 0:1], scale=1.0),
             reads=[b_stat[si], b_eps], writes=[b_stat[si]])
        P.op("dve", lambda e: e.reciprocal(out=st, in_=st), reads=[b_stat[si]], writes=[b_stat[si]])
        return st

    def norm_transpose(B, src, b_src, i, si=0):
        st = rms_rstd(B, src, b_src, si)
        P.op("dve", lambda e: e.scalar_tensor_tensor(out=B.xn[:], in0=src[:], scalar=st, in1=B.gpre[:],
                                                     op0=ALU.mult, op1=ALU.mult),
             reads=[b_src, b_stat[si], B.b_gpre], writes=[B.b_xn])
        pT = psb(0, 2)
        for c in range(16):
            P.op("pe", lambda e, c=c: e.transpose(out=pT[:, c * 128:(c + 1) * 128],
                                                  in_=B.xn[:, c * 128:(c + 1) * 128], identity=ident_b[:]),
                 reads=[B.b_xn, b_const], writes=[b_ps[0], b_ps[1]], inc=(c == 15))
        pv = pT.rearrange("p (c t) -> p c t", c=16)
        P.op("act", lambda e: e.copy(out=B.uT[:, 0:8, i * 128:(i + 1) * 128], in_=pv[:, 0:8, :]),
             reads=[b_ps[0]], writes=[B.b_uT[i]])
        P.op("dve", lambda e: e.tensor_copy(out=B.uT[:, 8:16, i * 128:(i + 1) * 128], in_=pv[:, 8:16, :]),
             reads=[b_ps[1]], writes=[B.b_uT[i]])

    def gemm_mov(B, w_d, njg, kg, lhs_fn, lhs_bufs_fn, evac_fn, w_c=None, wkey=None, first=True, pump=None):
        for cb in range(4):
            for jg in range(njg):
                s = B.wmv_n % 2
                B.wmv_n += 1
                wload(B.wmv[s][:, 0:kg, :], B.b_wmv[s], w_d[cb, jg], w_c[cb, jg], (wkey, cb, jg), first)
                for i in range(4):
                    for jj in range(kg):
                        k = jg * kg + jj
                        P.op("pe", lambda e, s=s, i=i, jj=jj, k=k, jg=jg: e.matmul(
                            ps[:, 4 + i, :], lhsT=lhs_fn(k, i), rhs=B.wmv[s][:, jj, :],
                            start=(jg == 0 and jj == 0), stop=(jg == njg - 1 and jj == kg - 1)),
                            reads=[B.b_wmv[s]] + lhs_bufs_fn(k, i), writes=[b_ps[4 + i]], inc=(jj == kg - 1))
                if pump is not None:
                    pump()
            for i in range(4):
                evac_fn(i, cb)

    def post_residual(B, i, resid_ap, b_resid, coef, si):
        st = rms_rstd(B, B.ft[i], B.b_ft[i], si)
        xs = i % 2
        P.dma("sp", lambda e: e.dma_start(out=B.xin[xs][:], in_=resid_ap), B.b_xin[xs],
              reads=[b_resid], writes=[B.b_xin[xs]])
        P.op("dve", lambda e: e.scalar_tensor_tensor(out=B.ft[i][:], in0=B.ft[i][:], scalar=st, in1=B.gpost[:],
                                                     op0=ALU.mult, op1=ALU.mult),
             reads=[B.b_ft[i], b_stat[si], B.b_gpost], writes=[B.b_ft[i]])
        P.op("dve", lambda e: e.scalar_tensor_tensor(out=B.ft[i][:], in0=B.ft[i][:], scalar=float(coef),
                                                     in1=B.xin[xs][:], op0=ALU.mult, op1=ALU.add),
             reads=[B.b_ft[i], B.b_xin[xs]], writes=[B.b_ft[i]])

    def ffn_stage(B, li, src_ap_fn, b_src_fn, gidx, first=True):
        load_gain(B, "pre", gidx)
        load_gain(B, "post", gidx + 1)
        for i in range(4):
            xs = i % 2
            P.dma("sp", lambda e, i=i, xs=xs: e.dma_start(out=B.xin[xs][:], in_=src_ap_fn(i)), B.b_xin[xs],
                  reads=[b_src_fn(i)], writes=[B.b_xin[xs]])
            norm_transpose(B, B.xin[xs], B.b_xin[xs], i, si=i % 2)
        for j in range(NJ):
            s = B.wgu_n % 3
            B.wgu_n += 1
            wload(B.wgu[s][:, 0], B.b_wgu[s], wg_d[li][j], wg_c[li][j], ("g", li, j), first)
            wload(B.wgu[s][:, 1], B.b_wgu[s], wu_d[li][j], wu_c[li][j], ("u", li, j), first)
            gb, ub = j % 2, 2 + j % 2
            for which, bank in ((0, gb), (1, ub)):
                for k in range(16):
                    P.op("pe", lambda e, s=s, which=which, bank=bank, k=k: e.matmul(
                        ps[:, bank, :], lhsT=B.wgu[s][:, which, k, :], rhs=B.uT[:, k, :],
                        start=(k == 0), stop=(k == 15)),
                        reads=[B.b_wgu[s]] + B.b_uT, writes=[b_ps[bank]], inc=(k == 15))
            sgs = j % 2
            P.op("act", lambda e, gb=gb, sgs=sgs: e.activation(out=B.sg[sgs][:], in_=ps[:, gb, :], func=AF.Silu),
                 reads=[b_ps[gb]], writes=[B.b_sg[sgs]])
            P.op("dve", lambda e, ub=ub, sgs=sgs, j=j: e.tensor_tensor(out=B.aT[:, j, :], in0=ps[:, ub, :],
                                                                      in1=B.sg[sgs][:], op=ALU.mult),
                 reads=[b_ps[ub], B.b_sg[sgs]], writes=[B.b_aT[j]])

        def evac(i, cb):
            eng = "act" if (i + cb) % 2 == 0 else "dve"
            if eng == "act":
                P.op("act", lambda e: e.copy(out=B.ft[i][:, cb * 512:(cb + 1) * 512], in_=ps[:, 4 + i, :]),
                     reads=[b_ps[4 + i]], writes=[B.b_ft[i]])
            else:
                P.op("dve", lambda e: e.tensor_copy(out=B.ft[i][:, cb * 512:(cb + 1) * 512], in_=ps[:, 4 + i, :]),
                     reads=[b_ps[4 + i]], writes=[B.b_ft[i]])
        gemm_mov(B, wd_d[li], 4, 11,
                 lambda k, i: B.aT[:, k, i * 128:(i + 1) * 128],
                 lambda k, i: [B.b_aT[k]], evac, w_c=wd_c[li], wkey=("d", li), first=first)
        for i in range(4):
            post_residual(B, i, src_ap_fn(i), b_src_fn(i), 0.5, si=i % 2)

    b_x = Buf("x_in")
    import os as _os
    n_tt1 = 0 if _os.environ.get('SKIP_P1') else NTOK // T

    def run_unit(u):
        for f in u:
            f()

    def partA(B, src, b_src, si):
        st = rms_rstd(B, src, b_src, si)
        P.op("dve", lambda e: e.tensor_scalar(out=B.xn[:], in0=src[:], scalar1=st, scalar2=1.0,
                                              op0=ALU.mult, op1=ALU.mult),
             reads=[b_src, b_stat[si]], writes=[B.b_xn])

    def partB(B, ub, i, g, bank0):
        pT = psb(bank0, 2)
        for c in range(16):
            P.op("pe", lambda e, c=c: e.transpose(out=pT[:, c * 128:(c + 1) * 128],
                                                  in_=B.xn[:, c * 128:(c + 1) * 128], identity=ident_b[:]),
                 reads=[B.b_xn, b_const], writes=[b_ps[bank0], b_ps[bank0 + 1]], inc=(c == 15))
        pv = pT.rearrange("p (c t) -> p c t", c=16)
        for c in range(16):
            dst = B.uTs[ub][:, c, i * 128:(i + 1) * 128]
            gcol = B.gT[:, g * 16 + c:g * 16 + c + 1]
            if c < 8:
                P.op("act", lambda e, c=c, dst=dst, gcol=gcol: e.activation(out=dst, in_=pv[:, c, :], func=AF.Copy,
                                                                            scale=gcol),
                     reads=[b_ps[bank0], B.b_gT], writes=[B.b_uTs[ub][i]])
            else:
                P.op("dve", lambda e, c=c, dst=dst, gcol=gcol: e.tensor_scalar(out=dst, in0=pv[:, c, :], scalar1=gcol,
                                                                              scalar2=1.0, op0=ALU.mult, op1=ALU.mult),
                     reads=[b_ps[bank0 + 1], B.b_gT], writes=[B.b_uTs[ub][i]])

    def evac_to_ft(B):
        def evac(i, cb):
            if (i + cb) % 2 == 0:
                P.op("act", lambda e: e.copy(out=B.ft[i][:, cb * 512:(cb + 1) * 512], in_=ps[:, 4 + i, :]),
                     reads=[b_ps[4 + i]], writes=[B.b_ft[i]])
            else:
                P.op("dve", lambda e: e.tensor_copy(out=B.ft[i][:, cb * 512:(cb + 1) * 512], in_=ps[:, 4 + i, :]),
                     reads=[b_ps[4 + i]], writes=[B.b_ft[i]])
        return evac

    def ffn_pipeline(B, li, ntiles, src_fn, g_pre, gpost_idx, post_store, g_u2, u2_store):
        P.dma("sp", lambda e: e.dma_start(out=B.gT[:], in_=gainsT_d[:, :]), B.b_gT, writes=[B.b_gT])
        load_gain(B, "post", gpost_idx)

        def n_units(t):
            ub = t % 2

            def L(i):
                xs = i % 2
                ap, bsrc = src_fn(t, i)
                P.dma("sp", lambda e: e.dma_start(out=B.xin[xs][:], in_=ap), B.b_xin[xs], reads=[bsrc],
                      writes=[B.b_xin[xs]])

            def A(i):
                partA(B, B.xin[i % 2], B.b_xin[i % 2], i % 2)

            def Bp(i):
                partB(B, ub, i, g_pre, 0)
            return [[lambda: L(0), lambda: L(1), lambda: A(0)],
                    [lambda: Bp(0), lambda: A(1), lambda: L(2)],
                    [lambda: Bp(1), lambda: A(2), lambda: L(3)],
                    [lambda: Bp(2), lambda: A(3)],
                    [lambda: Bp(3)]]

        def post_units(t):
            ub = t % 2

            def Pst(i):
                ap, bsrc = src_fn(t, i)
                post_residual(B, i, ap, bsrc, 0.5, si=2 + i % 2)
                post_store(B, t, i)

            def A(i):
                partA(B, B.ft[i], B.b_ft[i], 2 + i % 2)

            def Bp(i):
                partB(B, ub, i, g_u2, 6)
            return [[lambda: Pst(0)], [lambda: Pst(1), lambda: A(0)], [lambda: Pst(2), lambda: Bp(0), lambda: A(1)],
                    [lambda: Pst(3), lambda: Bp(1), lambda: A(2)], [lambda: Bp(2), lambda: A(3)],
                    [lambda: Bp(3), lambda: u2_store(B, t, ub)]]

        def gate_up(t, queue, first):
            ub = t % 2
            for j in range(NJ):
                s_ = B.wgu_n % 3
                B.wgu_n += 1
                wload(B.wgu[s_][:, 0], B.b_wgu[s_], wg_d[li][j], wg_c[li][j], ("g", li, j), first)
                wload(B.wgu[s_][:, 1], B.b_wgu[s_], wu_d[li][j], wu_c[li][j], ("u", li, j), first)
                gb, ubk = j % 2, 2 + j % 2
                for which, bank in ((0, gb), (1, ubk)):
                    for k in range(16):
                        P.op("pe", lambda e, s_=s_, which=which, bank=bank, k=k: e.matmul(
                            ps[:, bank, :], lhsT=B.wgu[s_][:, which, k, :], rhs=B.uTs[ub][:, k, :],
                            start=(k == 0), stop=(k == 15)),
                            reads=[B.b_wgu[s_]] + B.b_uTs[ub], writes=[b_ps[bank]], inc=(k == 15))
                sgs = j % 2
                P.op("act", lambda e, gb=gb, sgs=sgs: e.activation(out=B.sg[sgs][:], in_=ps[:, gb, :], func=AF.Silu),
                     reads=[b_ps[gb]], writes=[B.b_sg[sgs]])
                P.op("dve", lambda e, ubk=ubk, sgs=sgs, j=j: e.tensor_tensor(out=B.aT[:, j, :], in0=ps[:, ubk, :],
                                                                            in1=B.sg[sgs][:], op=ALU.mult),
                     reads=[b_ps[ubk], B.b_sg[sgs]], writes=[B.b_aT[j]])
                if queue and j >= 1 and j % 4 == 1:
                    run_unit(queue.pop(0))
            while queue:
                run_unit(queue.pop(0))

        if ntiles:
            for u in n_units(0):
                run_unit(u)
        pend = []
        for tt in range(ntiles):
            gate_up(tt, pend, tt == 0)
            qd = n_units(tt + 1) if tt + 1 < ntiles else []
            cnt = [0]

            def pump_d(qd=qd, cnt=cnt):
                cnt[0] += 1
                if qd and cnt[0] >= 3:
                    run_unit(qd.pop(0))
            gemm_mov(B, wd_d[li], 4, 11, lambda k, i: B.aT[:, k, i * 128:(i + 1) * 128], lambda k, i: [B.b_aT[k]],
                     evac_to_ft(B), w_c=wd_c[li], wkey=("d", li), first=(tt == 0), pump=pump_d)
            while qd:
                run_unit(qd.pop(0))
            pend = post_units(tt)
        while pend:
            run_unit(pend.pop(0))

    def p1_src(t, i):
        return x_d[t * T + i * 128: t * T + (i + 1) * 128, :], b_x

    def p1_post_store(B, t, i):
        tok0 = t * T + i * 128
        if OWN0 <= tok0 < OWN0 + NOWN:
            oi = (tok0 - OWN0) // 128
            P.dma("sp", lambda e: e.dma_start(out=h1_d[oi * 128:(oi + 1) * 128, :], in_=B.ft[i][:]),
                  B.b_ft[i], reads=[B.b_ft[i]], writes=[b_h1[oi]])

    def p1_u2_store(B, t, ub):
        P.dma("sp", lambda e: e.dma_start(out=u2T_d[:, :, t * T:(t + 1) * T].rearrange("k p t -> p k t"),
                                          in_=B.uTs[ub][:]),
              B.b_uTs[ub][0], reads=B.b_uTs[ub], writes=[b_u2T[t]])

    ffn_pipeline(alloc_ffn_bufs(pipelined=True), 0, n_tt1, p1_src, 0, 1, p1_post_store, 1, p1_u2_store)
    if stop_phase <= 1:
        return finish(P, nc, [b_u2T[-1]] + b_h1)
    P.barrier()
    P.sb_reset(arena0)

    HT = NTOK // 2
    u2h = P.sb("u2h", [128, 16, HT], BF16)
    b_u2h = [Buf("u2h%d" % i) for i in range(4)]
    wq = [P.sb("wq", [128, 16, 128], BF16) for _ in range(3)]
    b_wq = [Buf("wq%d" % i) for i in range(3)]
    cosb = P.sb("cosb", [128, HT], F32)
    sinb = P.sb("sinb", [128, HT], F32)
    b_cs = Buf("cossin")
    ost = [P.sb("ost", [128, HT], BF16) for _ in range(2)]
    b_ost = [Buf("ost%d" % i) for i in range(2)]
    xb = [P.sb("xb", [128, 512], BF16) for _ in range(4)]
    b_xb = [Buf("xb%d" % i) for i in range(4)]
    t1 = [P.sb("t1", [128, 512], F32) for _ in range(4)]
    b_t1 = [Buf("t1_%d" % i) for i in range(4)]
    t2 = [P.sb("t2", [128, 512], F32) for _ in range(4)]
    b_t2 = [Buf("t2_%d" % i) for i in range(4)]
    chunk_map = []
    for c in range(48):
        if c < 4:
            chunk_map.append(("q", c, False))
        elif c < 8:
            chunk_map.append(("k", c - 4, False))
        elif c < 12:
            chunk_map.append(("v", c - 8, False))
        elif c < 24:
            chunk_map.append(("q", 4 + c - 12, True))
        elif c < 36:
            chunk_map.append(("k", 4 + c - 24, True))
        else:
            chunk_map.append(("v", 4 + c - 36, False))
    nwq = 0
    nost = 0
    nrope = 0
    npsq = 0
    import os as _os
    _nh = int(_os.environ.get('PH2_HALVES', '2'))
    _cl = [int(v) for v in _os.environ.get('PH2_CHUNKS', ','.join(map(str, range(48)))).split(',') if v != '']
    for half in range(_nh):
        for t4 in range(4):
            tt = half * 4 + t4
            P.dma("sp", lambda e, tt=tt, t4=t4: e.dma_start(
                out=u2h[:, :, t4 * T:(t4 + 1) * T], in_=u2T_d[:, :, tt * T:(tt + 1) * T].rearrange("k p t -> p k t")),
                b_u2h[t4], reads=[b_u2T[tt]], writes=[b_u2h[t4]])
        P.dma("sp", lambda e, half=half: e.dma_start(out=cosb[:], in_=cos_d[:, half * HT:(half + 1) * HT]),
              b_cs, writes=[b_cs])
        P.dma("sp", lambda e, half=half: e.dma_start(out=sinb[:], in_=sin_d[:, half * HT:(half + 1) * HT]),
              b_cs, writes=[b_cs])
        for c in _cl:
            kind, head, rope = chunk_map[c]
            s = nwq % 3
            nwq += 1
            P.dma("pool", lambda e, s=s, c=c: e.dma_start(out=wq[s][:], in_=wqkv_d[c]), b_wq[s], writes=[b_wq[s]])
            if kind == "q":
                tiles = [2, 3] if half == 0 else [0, 1]
            else:
                tiles = [0, 1, 2, 3]
            os_ = nost % 2
            nost += 1
            for t4 in tiles:
                bank = npsq % 4
                npsq += 1
                for k in range(16):
                    P.op("pe", lambda e, s=s, k=k, t4=t4, bank=bank: e.matmul(
                        ps[:, bank, :], lhsT=wq[s][:, k, :], rhs=u2h[:, k, t4 * T:(t4 + 1) * T],
                        start=(k == 0), stop=(k == 15)),
                        reads=[b_wq[s], b_u2h[t4]], writes=[b_ps[bank]], inc=(k == 15))
                dst = ost[os_][:, t4 * T:(t4 + 1) * T]
                if not rope and kind == "q":
                    P.op("act", lambda e, dst=dst, bank=bank: e.activation(out=dst, in_=ps[:, bank, :], func=AF.Copy,
                                                                           scale=float(128 ** -0.5)),
                         reads=[b_ps[bank]], writes=[b_ost[os_]])
                elif not rope:
                    if npsq % 2 == 0:
                        P.op("act", lambda e, dst=dst, bank=bank: e.copy(out=dst, in_=ps[:, bank, :]),
                             reads=[b_ps[bank]], writes=[b_ost[os_]])
                    else:
                        P.op("dve", lambda e, dst=dst, bank=bank: e.tensor_copy(out=dst, in_=ps[:, bank, :]),
                             reads=[b_ps[bank]], writes=[b_ost[os_]])
                else:
                    rs = nrope % 4
                    nrope += 1
                    rbank = 4 + rs
                    P.op("act", lambda e, rs=rs, bank=bank: e.copy(out=xb[rs][:], in_=ps[:, bank, :]),
                         reads=[b_ps[bank]], writes=[b_xb[rs]])
                    P.op("pe", lambda e, rs=rs, rbank=rbank: e.matmul(ps[:, rbank, :], lhsT=rot_b[:], rhs=xb[rs][:],
                                                                      start=True, stop=True),
                         reads=[b_xb[rs], b_const], writes=[b_ps[rbank]])
                    P.op("dve", lambda e, rs=rs, bank=bank, t4=t4: e.tensor_tensor(
                        out=t1[rs][:], in0=ps[:, bank, :], in1=cosb[:, t4 * T:(t4 + 1) * T], op=ALU.mult),
                        reads=[b_ps[bank], b_cs, b_xb[rs]], writes=[b_t1[rs]])
                    P.op("dve", lambda e, rs=rs, rbank=rbank, t4=t4: e.tensor_tensor(
                        out=t2[rs][:], in0=ps[:, rbank, :], in1=sinb[:, t4 * T:(t4 + 1) * T], op=ALU.mult),
                        reads=[b_ps[rbank], b_cs], writes=[b_t2[rs]])
                    P.op("dve", lambda e, rs=rs, dst=dst: e.tensor_tensor(out=dst, in0=t1[rs][:], in1=t2[rs][:],
                                                                          op=ALU.add),
                         reads=[b_t1[rs], b_t2[rs]], writes=[b_ost[os_]])
            if kind == "q":
                lo = tiles[0] * T
                dlo = 0 if half == 0 else NOWN // 2
                P.dma("sp", lambda e, os_=os_, head=head, lo=lo, dlo=dlo: e.dma_start(
                    out=qT_d[head, :, dlo:dlo + 2 * T], in_=ost[os_][:, lo:lo + 2 * T]),
                    b_ost[os_], reads=[b_ost[os_]], writes=[b_qT[head]])
            else:
                dd, bb = (kT_d, b_kT) if kind == "k" else (vT_d, b_vT)
                P.dma("sp", lambda e, os_=os_, head=head, dd=dd, half=half: e.dma_start(
                    out=dd[head, :, half * HT:(half + 1) * HT], in_=ost[os_][:]),
                    b_ost[os_], reads=[b_ost[os_]], writes=[bb[head]])
    if stop_phase <= 2:
        return finish(P, nc, b_qT + b_kT + b_vT + b_h1)
    P.barrier()
    P.sb_reset(arena0)

    qs_ = [P.sb("qs", [128, NOWN], BF16) for _ in range(2)]
    ks_ = [P.sb("ks", [128, NTOK], BF16) for _ in range(2)]
    vs_ = [P.sb("vs", [128, NTOK], BF16) for _ in range(2)]
    b_qs = [Buf("qs%d" % i) for i in range(2)]
    b_ks = [Buf("ks%d" % i) for i in range(2)]
    b_vs = [Buf("vs%d" % i) for i in range(2)]
    vt = P.sb("vt", [128, 3, 32, 128], BF16)
    b_vt = [Buf("vt%d" % i) for i in range(3)]
    oacc = P.sb("oacc", [128, 2, NOWN], F32)
    b_oacc = Buf("oacc")
    mdil = P.sb("mdil", [128, 3 * 384 + 256], BF16)
    b_mdil = Buf("mdil")
    natab = P.sb("natab", [128, 5 * 896], BF16)
    b_natab = Buf("natab")
    et = [P.sb("et", [128, 896], BF16) for _ in range(3)]
    b_et = [Buf("et%d" % i) for i in range(3)]
    sq = P.sb("sq", [128, NOWN], BF16)
    b_sq = Buf("sq")
    rs_t = P.sb("rs_t", [128, NOWN], F32)
    b_rs = Buf("rs_t")
    ob = [P.sb("ob", [128, NOWN], BF16) for _ in range(2)]
    b_ob = [Buf("ob%d" % i) for i in range(2)]
    outg = P.sb("outg", [128, 16], F32)
    b_outg = Buf("outg")
    P.dma("sp", lambda e: e.dma_start(out=outg[:], in_=outg_d[:, :]), b_outg, writes=[b_outg])
    P.dma("pool", lambda e: e.dma_start(out=mdil[:], in_=mdil_d[:, :]), b_mdil, writes=[b_mdil])
    SCALE = float(128 ** -0.5)
    n_et = 0
    n_sb = 0
    n_yb = 0

    def load_head(h):
        s = h % 2
        P.dma("sp", lambda e: e.dma_start(out=qs_[s][:], in_=qT_d[h]), b_qs[s], reads=[b_qT[h]], writes=[b_qs[s]])
        P.dma("sp", lambda e: e.dma_start(out=ks_[s][:], in_=kT_d[h]), b_ks[s], reads=[b_kT[h]], writes=[b_ks[s]])
        P.dma("sp", lambda e: e.dma_start(out=vs_[s][:], in_=vT_d[h]), b_vs[s], reads=[b_vT[h]], writes=[b_vs[s]])

    def make_vt(s, fold, d, tile_list):
        per_res = 32 // d
        for g0 in range(0, len(tile_list), 8):
            grp = tile_list[g0:g0 + 8]
            bank = 6 + (g0 // 8) % 2
            pT = psb(bank, 1)
            for n, ti in enumerate(grp):
                r, mt = ti // per_res, ti % per_res
                start = r + d * 128 * mt
                src = vs_[s][:, start:start + d * 127 + 1:d]
                P.op("pe", lambda e, n=n, src=src, pT=pT: e.transpose(out=pT[:, n * 128:(n + 1) * 128], in_=src,
                                                                      identity=ident_b[:]),
                     reads=[b_vs[s], b_const], writes=[b_ps[bank]], inc=(n == len(grp) - 1))
            assert grp == list(range(grp[0], grp[0] + len(grp)))
            dst = vt[:, fold, grp[0]:grp[0] + len(grp), :]
            srcv = pT[:, 0:len(grp) * 128].rearrange("p (a b) -> p a b", b=128)
            if (g0 // 8) % 2 == 0:
                P.op("act", lambda e, dst=dst, srcv=srcv: e.copy(out=dst, in_=srcv), reads=[b_ps[bank]],
                     writes=[b_vt[fold]])
            else:
                P.op("dve", lambda e, dst=dst, srcv=srcv: e.tensor_copy(out=dst, in_=srcv), reads=[b_ps[bank]],
                     writes=[b_vt[fold]])

    def attn_group(s, q_ap, key_aps, mask_aps, vt_aps, mask_buf, fold, o_ap, first, esc=None):
        nonlocal n_et, n_sb, n_yb
        nt = len(key_aps)
        esc = SCALE if esc is None else esc
        sb0 = 2 * (n_sb % 2)
        n_sb += 1
        es = n_et % 3
        n_et += 1
        yb = 4 + n_yb % 2
        n_yb += 1
        sflat = ps[:, sb0:sb0 + 2, :].rearrange("p a b -> p (a b)")
        for j in range(nt):
            reg = sflat[:, j * 128:(j + 1) * 128]
            bank = sb0 + (j * 128) // 512
            P.op("pe", lambda e, reg=reg, m=mask_aps[j]: e.matmul(reg, lhsT=ident_b[:], rhs=m, start=True, stop=False),
                 reads=[b_const, mask_buf], writes=[b_ps[bank]], inc=False)
            P.op("pe", lambda e, reg=reg, k=key_aps[j]: e.matmul(reg, lhsT=k, rhs=q_ap, start=False, stop=True),
                 reads=[b_ks[s], b_qs[s]], writes=[b_ps[bank]], inc=(j == nt - 1 or j == 3))
        w0 = min(nt, 4) * 128
        P.op("act", lambda e: e.activation(out=et[es][:, 0:w0], in_=sflat[:, 0:w0], func=AF.Exp, scale=esc),
             reads=[b_ps[sb0]], writes=[b_et[es]])
        if nt > 4:
            P.op("act", lambda e: e.activation(out=et[es][:, w0:nt * 128], in_=sflat[:, w0:nt * 128], func=AF.Exp,
                                               scale=esc),
                 reads=[b_ps[sb0 + 1]], writes=[b_et[es]])
        def pv_part():
            for j in range(nt):
                P.op("pe", lambda e, j=j: e.matmul(ps[:, yb, 0:128], lhsT=vt_aps[j],
                                                   rhs=et[es][:, j * 128:(j + 1) * 128],
                                                   start=(j == 0), stop=(j == nt - 1)),
                     reads=[b_vt[fold], b_et[es]], writes=[b_ps[yb]], inc=False)
            for j in range(nt):
                P.op("pe", lambda e, j=j: e.matmul(ps[:, yb, 128:256], lhsT=ones_b[:],
                                                   rhs=et[es][:, j * 128:(j + 1) * 128],
                                                   start=(j == 0), stop=(j == nt - 1)),
                     reads=[b_const, b_et[es]], writes=[b_ps[yb]], inc=(j == nt - 1))
            yv = ps[:, yb, 0:256].rearrange("p (a b) -> p a b", a=2)
            if first:
                P.op("dve", lambda e: e.tensor_copy(out=o_ap, in_=yv), reads=[b_ps[yb]], writes=[b_oacc])
            else:
                P.op("dve", lambda e: e.tensor_tensor(out=o_ap, in0=yv, in1=o_ap, op=ALU.add),
                     reads=[b_ps[yb], b_oacc], writes=[b_oacc])
        pending.append(pv_part)
        if len(pending) > 1:
            pending.pop(0)()

    pending = []

    def attn_flush():
        while pending:
            pending.pop(0)()

    load_head(0)
    for h in range(16):
        s = h % 2
        if h + 1 < 16:
            load_head(h + 1)
        if h < 4:
            P.dma("pool", lambda e, h=h: e.dma_start(out=natab[:], in_=natab_d[h]), b_natab, writes=[b_natab])
            make_vt(s, 0, 1, list(range(5, 27)))
            for j in range(16):
                var = 0 if j == 0 else 1 if j == 1 else 3 if j == 14 else 4 if j == 15 else 2
                q_ap = qs_[s][:, j * 128:(j + 1) * 128]
                kt0 = 5 + j
                key_aps = [ks_[s][:, (kt0 + n) * 128:(kt0 + n + 1) * 128] for n in range(7)]
                mask_aps = [natab[:, var * 896 + n * 128: var * 896 + (n + 1) * 128] for n in range(7)]
                vt_aps = [vt[:, 0, kt0 + n, :] for n in range(7)]
                o_ap = oacc[:, :, j * 128:(j + 1) * 128]
                attn_group(s, q_ap, key_aps, mask_aps, vt_aps, b_natab, 0, o_ap, True, esc=1.0)
        else:
            make_vt(s, 0, 1, list(range(7, 25)))
            for r in range(4):
                make_vt(s, 1, 4, list(range(r * 8 + 1, r * 8 + 7)))
            for r0 in range(0, 16, 4):
                make_vt(s, 2, 16, list(range(r0 * 2, r0 * 2 + 8)))
            for mt in range(8, 24):
                var = 0 if mt == 8 else 2 if mt == 23 else 1
                q_ap = qs_[s][:, (mt - 8) * 128:(mt - 7) * 128]
                key_aps = [ks_[s][:, (mt - 1 + n) * 128:(mt + n) * 128] for n in range(3)]
                mask_aps = [mdil[:, var * 384 + n * 128: var * 384 + (n + 1) * 128] for n in range(3)]
                vt_aps = [vt[:, 0, mt - 1 + n, :] for n in range(3)]
                o_ap = oacc[:, :, (mt - 8) * 128:(mt - 7) * 128]
                attn_group(s, q_ap, key_aps, mask_aps, vt_aps, b_mdil, 0, o_ap, True)
            for r in range(4):
                for mt in range(2, 6):
                    var = 0 if mt == 2 else 2 if mt == 5 else 1
                    q0 = r + 4 * 128 * mt - OWN0
                    q_ap = qs_[s][:, q0:q0 + 4 * 127 + 1:4]
                    key_aps = []
                    for n in range(3):
                        k0 = r + 4 * 128 * (mt - 1 + n)
                        key_aps.append(ks_[s][:, k0:k0 + 4 * 127 + 1:4])
                    mask_aps = [mdil[:, var * 384 + n * 128: var * 384 + (n + 1) * 128] for n in range(3)]
                    vt_aps = [vt[:, 1, r * 8 + mt - 1 + n, :] for n in range(3)]
                    o_ap = oacc[:, :, q0:q0 + 4 * 127 + 1:4]
                    attn_group(s, q_ap, key_aps, mask_aps, vt_aps, b_mdil, 1, o_ap, False)
            for r in range(16):
                q0 = r + 16 * 64 - OWN0
                q_ap = qs_[s][:, q0:q0 + 16 * 127 + 1:16]
                key_aps = []
                for n in range(2):
                    k0 = r + 16 * 128 * n
                    key_aps.append(ks_[s][:, k0:k0 + 16 * 127 + 1:16])
                mask_aps = [mdil[:, 1152 + n * 128: 1152 + (n + 1) * 128] for n in range(2)]
                vt_aps = [vt[:, 2, r * 2 + n, :] for n in range(2)]
                o_ap = oacc[:, :, q0:q0 + 16 * 127 + 1:16]
                attn_group(s, q_ap, key_aps, mask_aps, vt_aps, b_mdil, 2, o_ap, False)
        attn_flush()
        P.op("dve", lambda e: e.reciprocal(out=oacc[:, 1, :], in_=oacc[:, 1, :]), reads=[b_oacc], writes=[b_oacc])
        P.op("dve", lambda e: e.tensor_tensor(out=oacc[:, 0, :], in0=oacc[:, 0, :], in1=oacc[:, 1, :], op=ALU.mult),
             reads=[b_oacc], writes=[b_oacc])
        P.op("act", lambda e: e.activation(out=sq[:], in_=oacc[:, 0, :], func=AF.Square), reads=[b_oacc],
             writes=[b_sq])
        obs = h % 2
        for c4 in range(4):
            bank = 6 + c4 % 2
            P.op("pe", lambda e, c4=c4, bank=bank: e.matmul(ps[:, bank, :], lhsT=ones_b[:],
                                                            rhs=sq[:, c4 * 512:(c4 + 1) * 512], start=True, stop=True),
                 reads=[b_sq, b_const], writes=[b_ps[bank]])
            P.op("act", lambda e, c4=c4, bank=bank: e.activation(out=rs_t[:, c4 * 512:(c4 + 1) * 512],
                                                                 in_=ps[:, bank, :], func=AF.Sqrt,
                                                                 bias=eps_t[:, 0:1], scale=1.0 / 128),
                 reads=[b_ps[bank], b_eps], writes=[b_rs])
        P.op("dve", lambda e: e.reciprocal(out=rs_t[:], in_=rs_t[:]), reads=[b_rs], writes=[b_rs])
        P.op("dve", lambda e, h=h, obs=obs: e.scalar_tensor_tensor(out=ob[obs][:], in0=oacc[:, 0, :],
                                                                   scalar=outg[:, h:h + 1], in1=rs_t[:],
                                                                   op0=ALU.mult, op1=ALU.mult),
             reads=[b_oacc, b_outg, b_rs], writes=[b_ob[obs]])
        P.dma("sp", lambda e, h=h, obs=obs: e.dma_start(out=oT_d[h], in_=ob[obs][:]), b_ob[obs],
              reads=[b_ob[obs]], writes=[b_oT[h]])
    if stop_phase <= 3:
        return finish(P, nc, b_oT + b_h1)
    P.barrier()
    P.sb_reset(arena0)

    NT4 = NOWN // T
    u4T_d = dscr("u4Ts", [16, 128, NOWN], BF16)
    b_u4T = [Buf("u4T_%d" % i) for i in range(NT4)]

    class SB4:
        pass

    def alloc_small(ple):
        B = SB4()
        B.xin = [P.sb("xin", [128, D], F32) for _ in range(2)]
        B.b_xin = [Buf("xin%d" % i) for i in range(2)]
        B.ft = [P.sb("ft", [128, D], F32) for _ in range(4)]
        B.b_ft = [Buf("ft%d" % i) for i in range(4)]
        B.junk = P.sb("junk", [128, D], BF16)
        B.b_junk = Buf("junk")
        B.uTs = [P.sb("uT", [128, 16, T], BF16) for _ in range(2)]
        B.b_uTs = [[Buf("uT%d_%d" % (u, i)) for i in range(4)] for u in range(2)]
        B.wmv = [P.sb("wmv", [128, 16, 512], BF16) for _ in range(3)]
        B.b_wmv = [Buf("wmv%d" % i) for i in range(3)]
        B.gpost = P.sb("gpost", [128, D], F32)
        B.b_gpost = Buf("gpost")
        B.sg = [P.sb("sg", [128, 512], F32) for _ in range(2)]
        B.b_sg = [Buf("sg%d" % i) for i in range(2)]
        B.wmv_n = 0
        return B

    def proj_phase(B, ntiles, in_d, b_in, w_d, w_c, wkey, resid_fn, out_store, evac_fn=None, extra_fn=None):
        def load_u(t):
            ub = t % 2
            P.dma("sp", lambda e: e.dma_start(out=B.uTs[ub][:], in_=in_d[:, :, t * T:(t + 1) * T].rearrange(
                "h p t -> p h t")), B.b_uTs[ub][0], reads=b_in(t), writes=B.b_uTs[ub])

        def post_units(t):
            def Pst(i):
                ap, bsrc = resid_fn(t, i)
                post_residual(B, i, ap, bsrc, 1.0, si=i % 2)
                out_store(B, t, i)
            return [[lambda: Pst(0), lambda: Pst(1)], [lambda: Pst(2), lambda: Pst(3)]]
        if ntiles:
            load_u(0)
        pend = []
        for t in range(ntiles):
            ub = t % 2
            if t + 1 < ntiles:
                load_u(t + 1)
            if extra_fn is not None:
                extra_fn(B, t)
            for cb in range(4):
                s_ = B.wmv_n % 3
                B.wmv_n += 1
                wload(B.wmv[s_][:], B.b_wmv[s_], w_d[cb, 0], w_c[cb, 0], (wkey, cb), t == 0)
                for i in range(4):
                    for k in range(16):
                        P.op("pe", lambda e, s_=s_, i=i, k=k, ub=ub: e.matmul(
                            ps[:, 4 + i, :], lhsT=B.uTs[ub][:, k, i * 128:(i + 1) * 128], rhs=B.wmv[s_][:, k, :],
                            start=(k == 0), stop=(k == 15)),
                            reads=[B.b_wmv[s_], B.b_uTs[ub][i]], writes=[b_ps[4 + i]], inc=(k == 15))
                    if cb == 0 and pend and i % 2 == 1:
                        run_unit(pend.pop(0))
                while cb == 0 and pend:
                    run_unit(pend.pop(0))
                for i in range(4):
                    if evac_fn is None:
                        evac_to_ft(B)(i, cb)
                    else:
                        evac_fn(B, t, i, cb)
            pend = post_units(t)
        while pend:
            run_unit(pend.pop(0))

    B = alloc_small(False)
    load_gain(B, "post", 3)
    proj_phase(B, NT4, oT_d, lambda t: b_oT, wo_d, wo_c, "o",
               lambda t, i: (h1_d[(t * 4 + i) * 128:(t * 4 + i + 1) * 128, :], b_h1[t * 4 + i]),
               lambda B, t, i: P.dma("sp", lambda e: e.dma_start(
                   out=h2_d[(t * 4 + i) * 128:(t * 4 + i + 1) * 128, :], in_=B.ft[i][:]),
                   B.b_ft[i], reads=[B.b_ft[i]], writes=[b_h2[t * 4 + i]]))
    if stop_phase <= 4:
        pass
    P.barrier()
    P.sb_reset(arena0)

    def p4_src(t, i):
        oi = t * 4 + i
        return h2_d[oi * 128:(oi + 1) * 128, :], b_h2[oi]

    def p4_post_store(B, t, i):
        oi = t * 4 + i
        P.dma("sp", lambda e: e.dma_start(out=h3_d[oi * 128:(oi + 1) * 128, :], in_=B.ft[i][:]),
              B.b_ft[i], reads=[B.b_ft[i]], writes=[b_h3[oi]])

    def p4_u4_store(B, t, ub):
        P.dma("sp", lambda e: e.dma_start(out=u4T_d[:, :, t * T:(t + 1) * T].rearrange("k p t -> p k t"),
                                          in_=B.uTs[ub][:]),
              B.b_uTs[ub][0], reads=B.b_uTs[ub], writes=[b_u4T[t]])

    ffn_pipeline(alloc_ffn_bufs(pipelined=True), 1, NT4, p4_src, 2, 5, p4_post_store, 3, p4_u4_store)
    P.barrier()
    P.sb_reset(arena0)

    B = alloc_small(True)
    pin = [P.sb("pin", [128, 256], F32) for _ in range(2)]
    b_pin = [Buf("pin%d" % i) for i in range(2)]
    pnb = P.sb("pnb", [128, 256], BF16)
    b_pnb = Buf("pnb")
    pTt = [P.sb("pTt", [128, 2, T], BF16) for _ in range(2)]
    b_pTt = [[Buf("pTt%d_%d" % (u, i)) for i in range(4)] for u in range(2)]
    wpp_s = [P.sb("wpp_s", [128, 2, 512], BF16) for _ in range(2)]
    b_wpp = [Buf("wpp%d" % i) for i in range(2)]
    load_gain(B, "post", 7)

    def ple_extra(B, t):
        for i in range(4):
            oi = t * 4 + i
            pi = i % 2
            P.dma("sp", lambda e, pi=pi, oi=oi: e.dma_start(out=pin[pi][:], in_=p_d[oi * 128:(oi + 1) * 128, :]),
                  b_pin[pi], writes=[b_pin[pi]])
            P.op("dve", lambda e, pi=pi: e.tensor_copy(out=pnb[:], in_=pin[pi][:]), reads=[b_pin[pi]], writes=[b_pnb])
            pT = psb(2, 1)
            for c in range(2):
                P.op("pe", lambda e, c=c, pT=pT: e.transpose(out=pT[:, c * 128:(c + 1) * 128],
                                                             in_=pnb[:, c * 128:(c + 1) * 128], identity=ident_b[:]),
                     reads=[b_pnb, b_const], writes=[b_ps[2]], inc=(c == 1))
            P.op("act", lambda e, i=i, pT=pT, t=t: e.copy(out=pTt[t % 2][:, :, i * 128:(i + 1) * 128],
                                                          in_=pT[:, 0:256].rearrange("p (c t) -> p c t", c=2)),
                 reads=[b_ps[2]], writes=[b_pTt[t % 2][i]])

    def ple_evac(B, t, i, cb):
        ws = cb % 2
        pb = i % 2
        if i == 0:
            wload(wpp_s[ws][:], b_wpp[ws], wpp_d[cb, 0], wpp_c[cb, 0], ("pp", cb), t == 0)
        for c in range(2):
            P.op("pe", lambda e, c=c: e.matmul(ps[:, pb, :], lhsT=pTt[t % 2][:, c, i * 128:(i + 1) * 128],
                                               rhs=wpp_s[ws][:, c, :], start=(c == 0), stop=(c == 1)),
                 reads=[b_pTt[t % 2][i], b_wpp[ws]], writes=[b_ps[pb]], inc=(c == 1))
        sgs = i % 2
        P.op("act", lambda e: e.activation(out=B.sg[sgs][:], in_=ps[:, 4 + i, :], func=AF.Sigmoid),
             reads=[b_ps[4 + i]], writes=[B.b_sg[sgs]])
        P.op("dve", lambda e: e.tensor_tensor(out=B.ft[i][:, cb * 512:(cb + 1) * 512], in0=ps[:, pb, :],
                                              in1=B.sg[sgs][:], op=ALU.mult),
             reads=[b_ps[pb], B.b_sg[sgs]], writes=[B.b_ft[i]])

    proj_phase(B, NT4, u4T_d, lambda t: [b_u4T[t]], wpg_d, wpg_c, "pg",
               lambda t, i: (h3_d[(t * 4 + i) * 128:(t * 4 + i + 1) * 128, :], b_h3[t * 4 + i]),
               lambda B, t, i: P.dma("sp", lambda e: e.dma_start(
                   out=out_d[(t * 4 + i) * 128:(t * 4 + i + 1) * 128, :], in_=B.ft[i][:]),
                   B.b_ft[i], reads=[B.b_ft[i]], writes=[b_out[t * 4 + i]]),
               evac_fn=ple_evac, extra_fn=ple_extra)
    return finish(P, nc, b_out)


def finish(P, nc, bufs):
    P.barrier()
    P.emit()
    P.close()
    return nc


def lay_stat(W):
    K, Fo = W.shape
    return np.ascontiguousarray(W.reshape(K // 128, 128, Fo // 128, 128).transpose(2, 1, 0, 3))


def lay_mov(W, kg):
    K, Do = W.shape
    nj = K // 128
    return np.ascontiguousarray(W.reshape(nj // kg, kg, 128, Do // 512, 512).transpose(3, 0, 2, 1, 4))


def dil_masks(q):
    k = np.arange(128)[:, None]
    qq = np.arange(128)[None, :]
    tabs = []
    for var in range(3):
        t = np.zeros((128, 384), np.float32)
        for j in range(3):
            ok = np.abs(128 * (j - 1) + k - qq) <= 64
            if var == 0 and j == 0 and q == 0:
                ok = np.zeros_like(ok)
            if var == 2 and j == 2 and q == 3:
                ok = np.zeros_like(ok)
            t[:, j * 128:(j + 1) * 128] = np.where(ok, 0.0, MASKV)
        tabs.append(t)
    t = np.zeros((128, 256), np.float32)
    for j in range(2):
        mk = 128 * j + k
        mq = 64 + qq
        ok = np.abs(mk - mq) <= 64
        if q == 0:
            ok = ok & (mk >= 64)
        if q == 3:
            ok = ok & (mk < 192)
        t[:, j * 128:(j + 1) * 128] = np.where(ok, 0.0, MASKV)
    tabs.append(t)
    return np.ascontiguousarray(np.concatenate(tabs, axis=1))


def na_tables(q, rpb):
    out = np.full((4, 128, 5, 7, 128), MASKV, np.float32)
    jsel = [0, 1, 7, 14, 15]
    kk = np.arange(128)
    k_ro, k_c = kk // 64, kk % 64
    qq = np.arange(128)
    q_ro, q_c = qq // 64, qq % 64
    for vi, j in enumerate(jsel):
        r0 = 32 * q + 2 * j
        rg = r0 + q_ro
        rs = np.clip(rg - 4, 0, 128 - 8)
        qs = np.clip(q_c - 8, 0, 64 - 16)
        for n in range(7):
            kr = r0 - 6 + 2 * n + k_ro
            ok = (kr[:, None] >= rs[None, :]) & (kr[:, None] < rs[None, :] + 8) & \
                 (k_c[:, None] >= qs[None, :]) & (k_c[:, None] < qs[None, :] + 16) & \
                 (kr[:, None] >= 0) & (kr[:, None] < 128)
            dr = np.clip(kr[:, None] - rg[None, :] + 7, 0, 14)
            dc = np.clip(k_c[:, None] - q_c[None, :] + 15, 0, 30)
            for h in range(4):
                out[h, :, vi, n, :] = np.where(ok, rpb[h][dr, dc], MASKV)
    return np.ascontiguousarray(out.reshape(4, 128, 5 * 896))


def rope_tables(q):
    g0 = q * NOWN - OWN0
    inv = np.float32(10000.0) ** (-np.arange(0, 128, 2, dtype=np.float32) / np.float32(128))
    s = (g0 + np.arange(NTOK)).astype(np.float32)
    ang = (s[:, None] * inv[None, :]).astype(np.float32)
    c = np.cos(ang).astype(np.float32).T
    sn = np.sin(ang).astype(np.float32).T
    return (np.ascontiguousarray(np.concatenate([c, c], 0)), np.ascontiguousarray(np.concatenate([sn, sn], 0)))


def make_consts():
    ident = np.eye(128, dtype=np.float32)
    rot = np.zeros((128, 128), np.float32)
    for m in range(64):
        rot[m + 64, m] = -1.0
        rot[m, m + 64] = 1.0
    ones = np.ones((128, 128), np.float32)
    return np.stack([ident, rot, ones])


def prep_inputs(inputs, cores=range(8)):
    f = lambda a: np.asarray(a, dtype=np.float32)
    x = f(inputs["x"])
    p = f(inputs["p"])[0]
    shared = {
        "wg1": lay_stat(f(inputs["ffn1_w_gate"])[0]), "wu1": lay_stat(f(inputs["ffn1_w_up"])[0]),
        "wd1": lay_mov(f(inputs["ffn1_w_down"])[0], 11),
        "wg2": lay_stat(f(inputs["ffn2_w_gate"])[0]), "wu2": lay_stat(f(inputs["ffn2_w_up"])[0]),
        "wd2": lay_mov(f(inputs["ffn2_w_down"])[0], 11),
        "wqkv": lay_stat(f(inputs["w_qkv"])[0]),
        "wo": lay_mov(f(inputs["w_o"])[0], 16),
        "wpg": lay_mov(f(inputs["w_ple_gate"])[0], 16),
        "wpp": lay_mov(f(inputs["w_ple_proj"])[0], 2),
        "gains": np.ascontiguousarray(np.stack([f(inputs[k])[0] for k in (
            "ffn1_pre_g", "ffn1_post_g", "mix_pre_g", "mix_post_g", "ffn2_pre_g", "ffn2_post_g",
            "ple_pre_g", "ple_post_g")])),
        "outg": np.ascontiguousarray(f(inputs["out_g"])[0].reshape(16, 128).T),
        "gainsT": np.ascontiguousarray(np.concatenate([f(inputs[k])[0].reshape(16, 128).T for k in (
            "ffn1_pre_g", "mix_pre_g", "ffn2_pre_g", "ple_pre_g")], axis=1)),
        "consts": make_consts(),
    }
    rpb = f(inputs["na_rpb"])[0]
    in_maps = []
    for c in cores:
        b, q = c // 4, c % 4
        g0 = q * NOWN - OWN0
        xh = np.zeros((NTOK, D), np.float32)
        lo, hi = max(g0, 0), min(g0 + NTOK, SEQ)
        xh[lo - g0:hi - g0] = x[b, lo:hi]
        cs, sn = rope_tables(q)
        m = dict(shared)
        m.update({"x": xh, "p": np.ascontiguousarray(p[b, q * NOWN:(q + 1) * NOWN]),
                  "cos": cs, "sin": sn, "mdil": dil_masks(q), "natab": na_tables(q, rpb)})
        in_maps.append(m)
    return in_maps


_NC_CACHE = {}


def kernel(**inputs):
    in_maps = prep_inputs(inputs)
    if "nc" not in _NC_CACHE:
        _NC_CACHE["nc"] = build_program()
    res = run_bass_kernel_spmd(_NC_CACHE["nc"], in_maps, core_ids=list(range(8)))
    out = np.zeros((2, SEQ, D), np.float32)
    for c in range(8):
        b, q = c // 4, c % 4
        out[b, q * NOWN:(q + 1) * NOWN] = res.results[c]["out"]
    return out
```

```python
import numpy as np
import concourse.bass as bass
import concourse.mybir as mybir
from concourse.bass_utils import run_bass_kernel_spmd

F32 = mybir.dt.float32
BF16 = mybir.dt.bfloat16
AF = mybir.ActivationFunctionType
ALU = mybir.AluOpType

ENGS = ("pe", "act", "dve", "pool", "sp")

D = 2048
DFF = 5632
NJ = DFF // 128
SEQ = 8192
NTOK = 4096
OWN0 = 1024
NOWN = 2048
T = 512
EPS = 1e-6
MASKV = -30000.0
SB_BASE = 16512
SB_CAP = 212800


class Buf:
    __slots__ = ("name", "w", "r", "sem", "semcnt")

    def __init__(self, name):
        self.name = name
        self.w = None
        self.r = {}
        self.sem = None
        self.semcnt = 0


class Prog:
    def __init__(self, nc):
        self.nc = nc
        self.ops = {e: [] for e in ENGS}
        self.cnt = {e: 0 for e in ENGS}
        self.sems = {}
        self.seen = {e: {} for e in ENGS}
        self._stack = []
        self.dma_bufs = []
        self.sb_off = SB_BASE
        self.nalloc = 0

    def new_sem(self, name):
        self.nsem = getattr(self, "nsem", 0) + 1
        cm = self.nc.semaphore("%s_%d" % (name, self.nsem))
        s = cm.__enter__()
        self._stack.append(cm)
        return s

    def eng_sem(self, e):
        if e not in self.sems:
            self.sems[e] = self.new_sem("s_" + e)
        return self.sems[e]

    def sb(self, name, shape, dtype):
        nbytes = int(np.prod(shape[1:])) * (4 if dtype == F32 else 2)
        nbytes = (nbytes + 63) // 64 * 64
        off = self.sb_off
        assert off + nbytes <= SB_BASE + SB_CAP, ("SBUF overflow", name, off + nbytes - SB_BASE)
        self.sb_off += nbytes
        self.nalloc += 1
        return self.nc.alloc_sbuf_tensor_at("%s_%d" % (name, self.nalloc), list(shape), dtype, offset=off)

    def sb_mark(self):
        return self.sb_off

    def sb_reset(self, mark):
        self.sb_off = mark

    def _deps(self, eng, reads, writes):
        need = {}

        def add(ev):
            if ev is None:
                return
            k, c = ev
            if need.get(k, 0) < c:
                need[k] = c
        for b in reads:
            add(b.w)
        for b in writes:
            add(b.w)
            for k, c in b.r.items():
                add((k, c))
        waits = []
        seen = self.seen[eng]
        for k, c in need.items():
            if k == "pe" and eng == "pe":
                continue
            if seen.get(k, 0) >= c:
                continue
            seen[k] = c
            waits.append((k, c))
        return waits

    def _sem_of(self, k):
        return self.eng_sem(k) if isinstance(k, str) else k

    def op(self, eng, fn, reads=(), writes=(), inc=True):
        waits = [(self._sem_of(k), c) for k, c in self._deps(eng, reads, writes)]
        if inc:
            self.cnt[eng] += 1
            ev = (eng, self.cnt[eng])
            sem = self.eng_sem(eng)
        else:
            ev = (eng, self.cnt[eng] + 1)
            sem = None

        def run(e, waits=waits, fn=fn, sem=sem):
            for s, c in waits:
                e.wait_ge(s, c)
            ins = fn(e)
            if sem is not None:
                ins.then_inc(sem, 1)
        self.ops[eng].append(run)
        for b in reads:
            if b.r.get(ev[0], 0) < ev[1]:
                b.r[ev[0]] = ev[1]
        for b in writes:
            b.w = ev
            b.r = {}
        return ev

    def dma(self, q, fn, owner, reads=(), writes=()):
        waits = [(self._sem_of(k), c) for k, c in self._deps(q, reads, writes)]
        if owner.sem is None:
            owner.sem = self.new_sem("d_" + owner.name)
            self.dma_bufs.append(owner)
        owner.semcnt += 16
        sem, c = owner.sem, owner.semcnt

        def run(e, waits=waits, fn=fn, sem=sem):
            for s, cc in waits:
                e.wait_ge(s, cc)
            fn(e).then_inc(sem, 16)
        self.ops[q].append(run)
        ev = (sem, c)
        for b in reads:
            if b.r.get(sem, 0) < c:
                b.r[sem] = c
        for b in writes:
            b.w = ev
            b.r = {}
        return ev

    def barrier(self):
        evs = [(e, self.cnt[e]) for e in ENGS if self.cnt[e] > 0]
        evs += [(b.sem, b.semcnt) for b in self.dma_bufs if b.semcnt > 0]
        for eng in ENGS:
            waits = []
            for k, c in evs:
                if k == eng:
                    continue
                if self.seen[eng].get(k, 0) >= c:
                    continue
                self.seen[eng][k] = c
                waits.append((self._sem_of(k), c))

            def run(e, waits=waits):
                for s, c in waits:
                    e.wait_ge(s, c)
            self.ops[eng].append(run)

    def emit(self):
        nc = self.nc
        with nc.Block() as block:
            @block.tensor
            def _(e):
                for f in self.ops["pe"]:
                    f(e)

            @block.scalar
            def _(e):
                for f in self.ops["act"]:
                    f(e)

            @block.vector
            def _(e):
                for f in self.ops["dve"]:
                    f(e)

            @block.gpsimd
            def _(e):
                for f in self.ops["pool"]:
                    f(e)

            @block.sync
            def _(e):
                for f in self.ops["sp"]:
                    f(e)

    def close(self):
        while self._stack:
            self._stack.pop().__exit__(None, None, None)


def build_program(stop_phase=4, dbg=False):
    nc = bass.Bass("TRN2", target_bir_lowering=False)
    P = Prog(nc)
    r0 = nc.bump_sbuf(SB_CAP)
    assert r0 is not None and r0[0] == SB_BASE, r0

    def din(name, shape, dt=F32):
        return nc.dram_tensor(name, list(shape), dt, kind="ExternalInput").ap()

    def dscr(name, shape, dt):
        kind = "ExternalOutput" if dbg else "Internal"
        return nc.dram_tensor(name, list(shape), dt, kind=kind).ap()

    x_d = din("x", [NTOK, D])
    p_d = din("p", [NOWN, 256])
    wg_d = [din("wg1", [NJ, 128, 16, 128]), din("wg2", [NJ, 128, 16, 128])]
    wu_d = [din("wu1", [NJ, 128, 16, 128]), din("wu2", [NJ, 128, 16, 128])]
    wd_d = [din("wd1", [4, 4, 128, 11, 512]), din("wd2", [4, 4, 128, 11, 512])]
    wqkv_d = din("wqkv", [48, 128, 16, 128])
    wo_d = din("wo", [4, 1, 128, 16, 512])
    wpg_d = din("wpg", [4, 1, 128, 16, 512])
    wpp_d = din("wpp", [4, 1, 128, 2, 512])
    gains_d = din("gains", [8, D])
    outg_d = din("outg", [128, 16])
    gainsT_d = din("gainsT", [128, 64])
    consts_d = din("consts", [3, 128, 128])
    cos_d = din("cos", [128, NTOK])
    sin_d = din("sin", [128, NTOK])
    mdil_d = din("mdil", [128, 3 * 384 + 256])
    natab_d = din("natab", [4, 128, 5 * 896])
    out_d = nc.dram_tensor("out", [NOWN, D], F32, kind="ExternalOutput").ap()

    h1_d = dscr("h1s", [NOWN, D], F32)
    h2_d = dscr("h2s", [NOWN, D], F32)
    h3_d = dscr("h3s", [NOWN, D], F32)
    u2T_d = dscr("u2Ts", [16, 128, NTOK], BF16)
    qT_d = dscr("qTs", [16, 128, NOWN], BF16)
    kT_d = dscr("kTs", [16, 128, NTOK], BF16)
    vT_d = dscr("vTs", [16, 128, NTOK], BF16)
    oT_d = dscr("oTs", [16, 128, NOWN], BF16)

    def dint(name, shape, dt):
        return nc.dram_tensor(name, list(shape), dt, kind="Internal").ap()
    wg_c = [dint("wg1c", [NJ, 128, 16, 128], BF16), dint("wg2c", [NJ, 128, 16, 128], BF16)]
    wu_c = [dint("wu1c", [NJ, 128, 16, 128], BF16), dint("wu2c", [NJ, 128, 16, 128], BF16)]
    wd_c = [dint("wd1c", [4, 4, 128, 11, 512], BF16), dint("wd2c", [4, 4, 128, 11, 512], BF16)]
    wo_c = dint("woc", [4, 1, 128, 16, 512], BF16)
    wpg_c = dint("wpgc", [4, 1, 128, 16, 512], BF16)
    wpp_c = dint("wppc", [4, 1, 128, 2, 512], BF16)
    cache_bufs = {}
    wb_bufs = {}

    def cbuf(key):
        if key not in cache_bufs:
            cache_bufs[key] = Buf("wc%d" % len(cache_bufs))
        return cache_bufs[key]

    def wload(dst_ap, slot_buf, w32_ap, wc_ap, key, first):
        if first:
            P.dma("pool", lambda e: e.dma_start(out=dst_ap, in_=w32_ap), slot_buf, writes=[slot_buf])
            if slot_buf.name not in wb_bufs:
                wb_bufs[slot_buf.name] = Buf("wb_" + slot_buf.name)
            P.dma("sp", lambda e: e.dma_start(out=wc_ap, in_=dst_ap), wb_bufs[slot_buf.name], reads=[slot_buf],
                  writes=[cbuf(key)])
        else:
            P.dma("pool", lambda e: e.dma_start(out=dst_ap, in_=wc_ap), slot_buf, reads=[cbuf(key)],
                  writes=[slot_buf])

    b_h1 = [Buf("h1_%d" % i) for i in range(16)]
    b_h2 = [Buf("h2_%d" % i) for i in range(16)]
    b_h3 = [Buf("h3_%d" % i) for i in range(16)]
    b_out = [Buf("out_%d" % i) for i in range(16)]
    b_u2T = [Buf("u2T_%d" % i) for i in range(8)]
    b_qT = [Buf("qT_%d" % i) for i in range(16)]
    b_kT = [Buf("kT_%d" % i) for i in range(16)]
    b_vT = [Buf("vT_%d" % i) for i in range(16)]
    b_oT = [Buf("oT_%d" % i) for i in range(16)]

    ps = nc.alloc_psum_tensor("ps", [128, 8, 512], F32)
    b_ps = [Buf("ps%d" % i) for i in range(8)]

    def psb(bank, n=1):
        return ps[:, bank:bank + n, :].bitcast(BF16).rearrange("p a b -> p (a b)")

    ident_b = P.sb("ident_b", [128, 128], BF16)
    rot_b = P.sb("rot_b", [128, 128], BF16)
    ones_b = P.sb("ones_b", [128, 128], BF16)
    eps_t = P.sb("eps_t", [128, 1], F32)
    stat = P.sb("stat", [128, 16], F32)
    b_const = Buf("const")
    b_eps = Buf("eps")
    P.dma("pool", lambda e: e.dma_start(out=ident_b[:], in_=consts_d[0]), b_const, writes=[b_const])
    P.dma("pool", lambda e: e.dma_start(out=rot_b[:], in_=consts_d[1]), b_const, writes=[b_const])
    P.dma("pool", lambda e: e.dma_start(out=ones_b[:], in_=consts_d[2]), b_const, writes=[b_const])
    P.op("dve", lambda e: e.memset(eps_t[:], EPS), writes=[b_eps])
    b_stat = [Buf("stat%d" % i) for i in range(4)]
    arena0 = P.sb_mark()

    class FFNBufs:
        pass

    def alloc_ffn_bufs(pipelined=False):
        B = FFNBufs()
        B.xin = [P.sb("xin", [128, D], F32) for _ in range(2)]
        B.b_xin = [Buf("xin%d" % i) for i in range(2)]
        B.ft = [P.sb("ft", [128, D], F32) for _ in range(4)]
        B.b_ft = [Buf("ft%d" % i) for i in range(4)]
        B.xn = P.sb("xn", [128, D], BF16)
        B.b_xn = Buf("xn")
        B.junk = P.sb("junk", [128, D], BF16)
        B.b_junk = Buf("junk")
        nu = 2 if pipelined else 1
        B.uTs = [P.sb("uT", [128, 16, T], BF16) for _ in range(nu)]
        B.b_uTs = [[Buf("uT%d_%d" % (u, i)) for i in range(4)] for u in range(nu)]
        B.uT, B.b_uT = B.uTs[0], B.b_uTs[0]
        B.aT = P.sb("aT", [128, NJ, T], BF16)
        B.b_aT = [Buf("aT%d" % i) for i in range(NJ)]
        B.wgu = [P.sb("wgu", [128, 2, 16, 128], BF16) for _ in range(3)]
        B.b_wgu = [Buf("wgu%d" % i) for i in range(3)]
        B.wmv = [P.sb("wmv", [128, 11 if pipelined else 16, 512], BF16) for _ in range(2)]
        B.b_wmv = [Buf("wmv%d" % i) for i in range(2)]
        if pipelined:
            B.gT = P.sb("gT", [128, 64], F32)
            B.b_gT = Buf("gT")
        else:
            B.gpre = P.sb("gpre", [128, D], F32)
            B.b_gpre = Buf("gpre")
        B.gpost = P.sb("gpost", [128, D], F32)
        B.b_gpost = Buf("gpost")
        B.sg = [P.sb("sg", [128, 512], F32) for _ in range(2)]
        B.b_sg = [Buf("sg%d" % i) for i in range(2)]
        B.wmv_n = 0
        B.wgu_n = 0
        return B

    def load_gain(B, which, idx):
        t, b = (B.gpre, B.b_gpre) if which == "pre" else (B.gpost, B.b_gpost)
        P.dma("sp", lambda e: e.dma_start(out=t[:], in_=gains_d[idx:idx + 1, :].partition_broadcast(128)),
              b, writes=[b])

    def rms_rstd(B, src, b_src, si):
        st = stat[:, si:si + 1]
        P.op("act", lambda e: e.activation(out=B.junk[:], in_=src[:], func=AF.Square,
                                           scale=float(D ** -0.5), accum_out=st),
             reads=[b_src], writes=[B.b_junk, b_stat[si]])
        P.op("act", lambda e: e.activation(out=st, in_=st, func=AF.Sqrt, bias=eps_t[:, 0:1], scale=1.0),
             reads=[b_stat[si], b_eps], writes=[b_stat[si]])
        P.op("dve", lambda e: e.reciprocal(out=st, in_=st), reads=[b_stat[si]], writes=[b_stat[si]])
        return st

    def norm_transpose(B, src, b_src, i, si=0):
        st = rms_rstd(B, src, b_src, si)
        P.op("dve", lambda e: e.scalar_tensor_tensor(out=B.xn[:], in0=src[:], scalar=st, in1=B.gpre[:],
                                                     op0=ALU.mult, op1=ALU.mult),
             reads=[b_src, b_stat[si], B.b_gpre], writes=[B.b_xn])
        pT = psb(0, 2)
        for c in range(16):
            P.op("pe", lambda e, c=c: e.transpose(out=pT[:, c * 128:(c + 1) * 128],
                                                  in_=B.xn[:, c * 128:(c + 1) * 128], identity=ident_b[:]),
                 reads=[B.b_xn, b_const], writes=[b_ps[0], b_ps[1]], inc=(c == 15))
        pv = pT.rearrange("p (c t) -> p c t", c=16)
        P.op("act", lambda e: e.copy(out=B.uT[:, 0:8, i * 128:(i + 1) * 128], in_=pv[:, 0:8, :]),
             reads=[b_ps[0]], writes=[B.b_uT[i]])
        P.op("dve", lambda e: e.tensor_copy(out=B.uT[:, 8:16, i * 128:(i + 1) * 128], in_=pv[:, 8:16, :]),
             reads=[b_ps[1]], writes=[B.b_uT[i]])

    def gemm_mov(B, w_d, njg, kg, lhs_fn, lhs_bufs_fn, evac_fn, w_c=None, wkey=None, first=True, pump=None):
        for cb in range(4):
            for jg in range(njg):
                s = B.wmv_n % 2
                B.wmv_n += 1
                wload(B.wmv[s][:, 0:kg, :], B.b_wmv[s], w_d[cb, jg], w_c[cb, jg], (wkey, cb, jg), first)
                for i in range(4):
                    for jj in range(kg):
                        k = jg * kg + jj
                        P.op("pe", lambda e, s=s, i=i, jj=jj, k=k, jg=jg: e.matmul(
                            ps[:, 4 + i, :], lhsT=lhs_fn(k, i), rhs=B.wmv[s][:, jj, :],
                            start=(jg == 0 and jj == 0), stop=(jg == njg - 1 and jj == kg - 1)),
                            reads=[B.b_wmv[s]] + lhs_bufs_fn(k, i), writes=[b_ps[4 + i]], inc=(jj == kg - 1))
                if pump is not None:
                    pump()
            for i in range(4):
                evac_fn(i, cb)

    def post_residual(B, i, resid_ap, b_resid, coef, si):
        st = rms_rstd(B, B.ft[i], B.b_ft[i], si)
        xs = i % 2
        P.dma("sp", lambda e: e.dma_start(out=B.xin[xs][:], in_=resid_ap), B.b_xin[xs],
              reads=[b_resid], writes=[B.b_xin[xs]])
        P.op("dve", lambda e: e.scalar_tensor_tensor(out=B.ft[i][:], in0=B.ft[i][:], scalar=st, in1=B.gpost[:],
                                                     op0=ALU.mult, op1=ALU.mult),
             reads=[B.b_ft[i], b_stat[si], B.b_gpost], writes=[B.b_ft[i]])
        P.op("dve", lambda e: e.scalar_tensor_tensor(out=B.ft[i][:], in0=B.ft[i][:], scalar=float(coef),
                                                     in1=B.xin[xs][:], op0=ALU.mult, op1=ALU.add),
             reads=[B.b_ft[i], B.b_xin[xs]], writes=[B.b_ft[i]])

    def ffn_stage(B, li, src_ap_fn, b_src_fn, gidx, first=True):
        load_gain(B, "pre", gidx)
        load_gain(B, "post", gidx + 1)
        for i in range(4):
            xs = i % 2
            P.dma("sp", lambda e, i=i, xs=xs: e.dma_start(out=B.xin[xs][:], in_=src_ap_fn(i)), B.b_xin[xs],
                  reads=[b_src_fn(i)], writes=[B.b_xin[xs]])
            norm_transpose(B, B.xin[xs], B.b_xin[xs], i, si=i % 2)
        for j in range(NJ):
            s = B.wgu_n % 3
            B.wgu_n += 1
            wload(B.wgu[s][:, 0], B.b_wgu[s], wg_d[li][j], wg_c[li][j], ("g", li, j), first)
            wload(B.wgu[s][:, 1], B.b_wgu[s], wu_d[li][j], wu_c[li][j], ("u", li, j), first)
            gb, ub = j % 2, 2 + j % 2
            for which, bank in ((0, gb), (1, ub)):
                for k in range(16):
                    P.op("pe", lambda e, s=s, which=which, bank=bank, k=k: e.matmul(
                        ps[:, bank, :], lhsT=B.wgu[s][:, which, k, :], rhs=B.uT[:, k, :],
                        start=(k == 0), stop=(k == 15)),
                        reads=[B.b_wgu[s]] + B.b_uT, writes=[b_ps[bank]], inc=(k == 15))
            sgs = j % 2
            P.op("act", lambda e, gb=gb, sgs=sgs: e.activation(out=B.sg[sgs][:], in_=ps[:, gb, :], func=AF.Silu),
                 reads=[b_ps[gb]], writes=[B.b_sg[sgs]])
            P.op("dve", lambda e, ub=ub, sgs=sgs, j=j: e.tensor_tensor(out=B.aT[:, j, :], in0=ps[:, ub, :],
                                                                      in1=B.sg[sgs][:], op=ALU.mult),
                 reads=[b_ps[ub], B.b_sg[sgs]], writes=[B.b_aT[j]])

        def evac(i, cb):
            eng = "act" if (i + cb) % 2 == 0 else "dve"
            if eng == "act":
                P.op("act", lambda e: e.copy(out=B.ft[i][:, cb * 512:(cb + 1) * 512], in_=ps[:, 4 + i, :]),
                     reads=[b_ps[4 + i]], writes=[B.b_ft[i]])
            else:
                P.op("dve", lambda e: e.tensor_copy(out=B.ft[i][:, cb * 512:(cb + 1) * 512], in_=ps[:, 4 + i, :]),
                     reads=[b_ps[4 + i]], writes=[B.b_ft[i]])
        gemm_mov(B, wd_d[li], 4, 11,
                 lambda k, i: B.aT[:, k, i * 128:(i + 1) * 128],
                 lambda k, i: [B.b_aT[k]], evac, w_c=wd_c[li], wkey=("d", li), first=first)
        for i in range(4):
            post_residual(B, i, src_ap_fn(i), b_src_fn(i), 0.5, si=i % 2)

    b_x = Buf("x_in")
    import os as _os
    n_tt1 = 0 if _os.environ.get('SKIP_P1') else NTOK // T

    def run_unit(u):
        for f in u:
            f()

    def partA(B, src, b_src, si):
        st = rms_rstd(B, src, b_src, si)
        P.op("dve", lambda e: e.tensor_scalar(out=B.xn[:], in0=src[:], scalar1=st, scalar2=1.0,
                                              op0=ALU.mult, op1=ALU.mult),
             reads=[b_src, b_stat[si]], writes=[B.b_xn])

    def partB(B, ub, i, g, bank0):
        pT = psb(bank0, 2)
        for c in range(16):
            P.op("pe", lambda e, c=c: e.transpose(out=pT[:, c * 128:(c + 1) * 128],
                                                  in_=B.xn[:, c * 128:(c + 1) * 128], identity=ident_b[:]),
                 reads=[B.b_xn, b_const], writes=[b_ps[bank0], b_ps[bank0 + 1]], inc=(c == 15))
        pv = pT.rearrange("p (c t) -> p c t", c=16)
        for c in range(16):
            dst = B.uTs[ub][:, c, i * 128:(i + 1) * 128]
            gcol = B.gT[:, g * 16 + c:g * 16 + c + 1]
            if c < 8:
                P.op("act", lambda e, c=c, dst=dst, gcol=gcol: e.activation(out=dst, in_=pv[:, c, :], func=AF.Copy,
                                                                            scale=gcol),
                     reads=[b_ps[bank0], B.b_gT], writes=[B.b_uTs[ub][i]])
            else:
                P.op("dve", lambda e, c=c, dst=dst, gcol=gcol: e.tensor_scalar(out=dst, in0=pv[:, c, :], scalar1=gcol,
                                                                              scalar2=1.0, op0=ALU.mult, op1=ALU.mult),
                     reads=[b_ps[bank0 + 1], B.b_gT], writes=[B.b_uTs[ub][i]])

    def evac_to_ft(B):
        def evac(i, cb):
            if (i + cb) % 2 == 0:
                P.op("act", lambda e: e.copy(out=B.ft[i][:, cb * 512:(cb + 1) * 512], in_=ps[:, 4 + i, :]),
                     reads=[b_ps[4 + i]], writes=[B.b_ft[i]])
            else:
                P.op("dve", lambda e: e.tensor_copy(out=B.ft[i][:, cb * 512:(cb + 1) * 512], in_=ps[:, 4 + i, :]),
                     reads=[b_ps[4 + i]], writes=[B.b_ft[i]])
        return evac

    def ffn_pipeline(B, li, ntiles, src_fn, g_pre, gpost_idx, post_store, g_u2, u2_store, cast_first=True):
        P.dma("sp", lambda e: e.dma_start(out=B.gT[:], in_=gainsT_d[:, :]), B.b_gT, writes=[B.b_gT])
        load_gain(B, "post", gpost_idx)

        def n_units(t):
            ub = t % 2

            def L(i):
                xs = i % 2
                ap, bsrc = src_fn(t, i)
                P.dma("sp", lambda e: e.dma_start(out=B.xin[xs][:], in_=ap), B.b_xin[xs], reads=[bsrc],
                      writes=[B.b_xin[xs]])

            def A(i):
                partA(B, B.xin[i % 2], B.b_xin[i % 2], i % 2)

            def Bp(i):
                partB(B, ub, i, g_pre, 0)
            return [[lambda: L(0), lambda: L(1), lambda: A(0)],
                    [lambda: Bp(0), lambda: A(1), lambda: L(2)],
                    [lambda: Bp(1), lambda: A(2), lambda: L(3)],
                    [lambda: Bp(2), lambda: A(3)],
                    [lambda: Bp(3)]]

        def post_units(t):
            ub = t % 2

            def Pst(i):
                ap, bsrc = src_fn(t, i)
                post_residual(B, i, ap, bsrc, 0.5, si=2 + i % 2)
                post_store(B, t, i)

            def A(i):
                partA(B, B.ft[i], B.b_ft[i], 2 + i % 2)

            def Bp(i):
                partB(B, ub, i, g_u2, 6)
            return [[lambda: Pst(0)], [lambda: Pst(1), lambda: A(0)], [lambda: Pst(2), lambda: Bp(0), lambda: A(1)],
                    [lambda: Pst(3), lambda: Bp(1), lambda: A(2)], [lambda: Bp(2), lambda: A(3)],
                    [lambda: Bp(3), lambda: u2_store(B, t, ub)]]

        def gate_up(t, queue, first):
            ub = t % 2
            for j in range(NJ):
                s_ = B.wgu_n % 3
                B.wgu_n += 1
                wload(B.wgu[s_][:, 0], B.b_wgu[s_], wg_d[li][j], wg_c[li][j], ("g", li, j), first)
                wload(B.wgu[s_][:, 1], B.b_wgu[s_], wu_d[li][j], wu_c[li][j], ("u", li, j), first)
                gb, ubk = j % 2, 2 + j % 2
                for which, bank in ((0, gb), (1, ubk)):
                    for k in range(16):
                        P.op("pe", lambda e, s_=s_, which=which, bank=bank, k=k: e.matmul(
                            ps[:, bank, :], lhsT=B.wgu[s_][:, which, k, :], rhs=B.uTs[ub][:, k, :],
                            start=(k == 0), stop=(k == 15)),
                            reads=[B.b_wgu[s_]] + B.b_uTs[ub], writes=[b_ps[bank]], inc=(k == 15))
                sgs = j % 2
                P.op("act", lambda e, gb=gb, sgs=sgs: e.activation(out=B.sg[sgs][:], in_=ps[:, gb, :], func=AF.Silu),
                     reads=[b_ps[gb]], writes=[B.b_sg[sgs]])
                P.op("dve", lambda e, ubk=ubk, sgs=sgs, j=j: e.tensor_tensor(out=B.aT[:, j, :], in0=ps[:, ubk, :],
                                                                            in1=B.sg[sgs][:], op=ALU.mult),
                     reads=[b_ps[ubk], B.b_sg[sgs]], writes=[B.b_aT[j]])
                if queue and j >= 1 and j % 4 == 1:
                    run_unit(queue.pop(0))
            while queue:
                run_unit(queue.pop(0))

        if ntiles:
            for u in n_units(0):
                run_unit(u)
        pend = []
        for tt in range(ntiles):
            gate_up(tt, pend, cast_first and tt == 0)
            qd = n_units(tt + 1) if tt + 1 < ntiles else []
            cnt = [0]

            def pump_d(qd=qd, cnt=cnt):
                cnt[0] += 1
                if qd and cnt[0] >= 3:
                    run_unit(qd.pop(0))
            gemm_mov(B, wd_d[li], 4, 11, lambda k, i: B.aT[:, k, i * 128:(i + 1) * 128], lambda k, i: [B.b_aT[k]],
                     evac_to_ft(B), w_c=wd_c[li], wkey=("d", li), first=(cast_first and tt == 0), pump=pump_d)
            while qd:
                run_unit(qd.pop(0))
            pend = post_units(tt)
        while pend:
            run_unit(pend.pop(0))

    def p1_src(t, i):
        return x_d[t * T + i * 128: t * T + (i + 1) * 128, :], b_x

    def p1_post_store(B, t, i):
        tok0 = t * T + i * 128
        if OWN0 <= tok0 < OWN0 + NOWN:
            oi = (tok0 - OWN0) // 128
            P.dma("sp", lambda e: e.dma_start(out=h1_d[oi * 128:(oi + 1) * 128, :], in_=B.ft[i][:]),
                  B.b_ft[i], reads=[B.b_ft[i]], writes=[b_h1[oi]])

    def p1_u2_store(B, t, ub):
        P.dma("sp", lambda e: e.dma_start(out=u2T_d[:, :, t * T:(t + 1) * T].rearrange("k p t -> p k t"),
                                          in_=B.uTs[ub][:]),
              B.b_uTs[ub][0], reads=B.b_uTs[ub], writes=[b_u2T[t]])

    ffn_pipeline(alloc_ffn_bufs(pipelined=True), 0, n_tt1, p1_src, 0, 1, p1_post_store, 1, p1_u2_store)
    if stop_phase <= 1:
        return finish(P, nc, [b_u2T[-1]] + b_h1)
    P.barrier()
    P.sb_reset(arena0)

    HT = NTOK // 2
    u2h = P.sb("u2h", [128, 16, HT], BF16)
    b_u2h = [Buf("u2h%d" % i) for i in range(4)]
    wq = [P.sb("wq", [128, 16, 128], BF16) for _ in range(3)]
    b_wq = [Buf("wq%d" % i) for i in range(3)]
    cosb = P.sb("cosb", [128, HT], F32)
    sinb = P.sb("sinb", [128, HT], F32)
    b_cs = Buf("cossin")
    ost = [P.sb("ost", [128, HT], BF16) for _ in range(2)]
    b_ost = [Buf("ost%d" % i) for i in range(2)]
    xb = [P.sb("xb", [128, 512], BF16) for _ in range(4)]
    b_xb = [Buf("xb%d" % i) for i in range(4)]
    t1 = [P.sb("t1", [128, 512], F32) for _ in range(4)]
    b_t1 = [Buf("t1_%d" % i) for i in range(4)]
    t2 = [P.sb("t2", [128, 512], F32) for _ in range(4)]
    b_t2 = [Buf("t2_%d" % i) for i in range(4)]
    chunk_map = []
    for c in range(48):
        if c < 4:
            chunk_map.append(("q", c, False))
        elif c < 8:
            chunk_map.append(("k", c - 4, False))
        elif c < 12:
            chunk_map.append(("v", c - 8, False))
        elif c < 24:
            chunk_map.append(("q", 4 + c - 12, True))
        elif c < 36:
            chunk_map.append(("k", 4 + c - 24, True))
        else:
            chunk_map.append(("v", 4 + c - 36, False))
    rope_pend = []

    def store_chunk(kind, os_, head, tiles, half):
        if kind == "q":
            lo = tiles[0] * T
            dlo = 0 if half == 0 else NOWN // 2
            P.dma("sp", lambda e: e.dma_start(out=qT_d[head, :, dlo:dlo + 2 * T], in_=ost[os_][:, lo:lo + 2 * T]),
                  b_ost[os_], reads=[b_ost[os_]], writes=[b_qT[head]])
        else:
            dd, bb = (kT_d, b_kT) if kind == "k" else (vT_d, b_vT)
            P.dma("sp", lambda e: e.dma_start(out=dd[head, :, half * HT:(half + 1) * HT], in_=ost[os_][:]),
                  b_ost[os_], reads=[b_ost[os_]], writes=[bb[head]])

    nwq = 0
    nost = 0
    nrope = 0
    npsq = 0
    import os as _os
    _nh = int(_os.environ.get('PH2_HALVES', '2'))
    _cl = [int(v) for v in _os.environ.get('PH2_CHUNKS', ','.join(map(str, range(48)))).split(',') if v != '']
    for half in range(_nh):
        for t4 in range(4):
            tt = half * 4 + t4
            P.dma("sp", lambda e, tt=tt, t4=t4: e.dma_start(
                out=u2h[:, :, t4 * T:(t4 + 1) * T], in_=u2T_d[:, :, tt * T:(tt + 1) * T].rearrange("k p t -> p k t")),
                b_u2h[t4], reads=[b_u2T[tt]], writes=[b_u2h[t4]])
        P.dma("sp", lambda e, half=half: e.dma_start(out=cosb[:], in_=cos_d[:, half * HT:(half + 1) * HT]),
              b_cs, writes=[b_cs])
        P.dma("sp", lambda e, half=half: e.dma_start(out=sinb[:], in_=sin_d[:, half * HT:(half + 1) * HT]),
              b_cs, writes=[b_cs])
        for c in _cl:
            kind, head, rope = chunk_map[c]
            if not rope:
                while rope_pend:
                    rope_pend.pop(0)()
            s = nwq % 3
            nwq += 1
            P.dma("pool", lambda e, s=s, c=c: e.dma_start(out=wq[s][:], in_=wqkv_d[c]), b_wq[s], writes=[b_wq[s]])
            if kind == "q":
                tiles = [2, 3] if half == 0 else [0, 1]
            else:
                tiles = [0, 1, 2, 3]
            os_ = nost % 2
            nost += 1
            for t4 in tiles:
                bank = npsq % 4
                npsq += 1
                for k in range(16):
                    P.op("pe", lambda e, s=s, k=k, t4=t4, bank=bank: e.matmul(
                        ps[:, bank, :], lhsT=wq[s][:, k, :], rhs=u2h[:, k, t4 * T:(t4 + 1) * T],
                        start=(k == 0), stop=(k == 15)),
                        reads=[b_wq[s], b_u2h[t4]], writes=[b_ps[bank]], inc=(k == 15))
                dst = ost[os_][:, t4 * T:(t4 + 1) * T]
                if not rope and kind == "q":
                    P.op("act", lambda e, dst=dst, bank=bank: e.activation(out=dst, in_=ps[:, bank, :], func=AF.Copy,
                                                                           scale=float(128 ** -0.5)),
                         reads=[b_ps[bank]], writes=[b_ost[os_]])
                elif not rope:
                    if npsq % 2 == 0:
                        P.op("act", lambda e, dst=dst, bank=bank: e.copy(out=dst, in_=ps[:, bank, :]),
                             reads=[b_ps[bank]], writes=[b_ost[os_]])
                    else:
                        P.op("dve", lambda e, dst=dst, bank=bank: e.tensor_copy(out=dst, in_=ps[:, bank, :]),
                             reads=[b_ps[bank]], writes=[b_ost[os_]])
                else:
                    rs = nrope % 4
                    nrope += 1
                    rbank = 4 + rs
                    P.op("act", lambda e, rs=rs, bank=bank: e.copy(out=xb[rs][:], in_=ps[:, bank, :]),
                         reads=[b_ps[bank]], writes=[b_xb[rs]])
                    P.op("dve", lambda e, rs=rs, bank=bank, t4=t4: e.tensor_tensor(
                        out=t1[rs][:], in0=ps[:, bank, :], in1=cosb[:, t4 * T:(t4 + 1) * T], op=ALU.mult),
                        reads=[b_ps[bank], b_cs, b_xb[rs]], writes=[b_t1[rs]])

                    def tail(rs=rs, rbank=rbank, t4=t4, dst=dst, os_=os_, last=(t4 == tiles[-1]), st=None):
                        P.op("pe", lambda e: e.matmul(ps[:, rbank, :], lhsT=rot_b[:], rhs=xb[rs][:],
                                                      start=True, stop=True),
                             reads=[b_xb[rs], b_const], writes=[b_ps[rbank]])
                        P.op("dve", lambda e: e.tensor_tensor(
                            out=t2[rs][:], in0=ps[:, rbank, :], in1=sinb[:, t4 * T:(t4 + 1) * T], op=ALU.mult),
                            reads=[b_ps[rbank], b_cs], writes=[b_t2[rs]])
                        P.op("dve", lambda e: e.tensor_tensor(out=dst, in0=t1[rs][:], in1=t2[rs][:], op=ALU.add),
                             reads=[b_t1[rs], b_t2[rs]], writes=[b_ost[os_]])
                    while rope_pend:
                        rope_pend.pop(0)()
                    rope_pend.append(tail)
                    if t4 == tiles[-1]:
                        rope_pend.append(lambda kind=kind, os_=os_, head=head, tiles=tiles, half=half:
                                         store_chunk(kind, os_, head, tiles, half))
            if not rope:
                store_chunk(kind, os_, head, tiles, half)
    while rope_pend:
        rope_pend.pop(0)()
    if stop_phase <= 2:
        return finish(P, nc, b_qT + b_kT + b_vT + b_h1)
    P.barrier()
    P.sb_reset(arena0)

    qs_ = [P.sb("qs", [128, NOWN], BF16) for _ in range(2)]
    ks_ = [P.sb("ks", [128, NTOK], BF16) for _ in range(2)]
    vs_ = [P.sb("vs", [128, NTOK], BF16) for _ in range(2)]
    b_qs = [Buf("qs%d" % i) for i in range(2)]
    b_ks = [Buf("ks%d" % i) for i in range(2)]
    b_vs = [Buf("vs%d" % i) for i in range(2)]
    vt = P.sb("vt", [128, 3, 32, 128], BF16)
    b_vt = [Buf("vt%d" % i) for i in range(3)]
    oacc = P.sb("oacc", [128, 2, NOWN], F32)
    b_oacc = Buf("oacc")
    mdil = P.sb("mdil", [128, 3 * 384 + 256], BF16)
    b_mdil = Buf("mdil")
    natab = P.sb("natab", [128, 5 * 896], BF16)
    b_natab = Buf("natab")
    et = [P.sb("et", [128, 896], BF16) for _ in range(3)]
    b_et = [Buf("et%d" % i) for i in range(3)]
    sq = P.sb("sq", [128, NOWN], BF16)
    b_sq = Buf("sq")
    rs_t = P.sb("rs_t", [128, NOWN], F32)
    b_rs = Buf("rs_t")
    ob = [P.sb("ob", [128, NOWN], BF16) for _ in range(2)]
    b_ob = [Buf("ob%d" % i) for i in range(2)]
    outg = P.sb("outg", [128, 16], F32)
    b_outg = Buf("outg")
    P.dma("sp", lambda e: e.dma_start(out=outg[:], in_=outg_d[:, :]), b_outg, writes=[b_outg])
    P.dma("pool", lambda e: e.dma_start(out=mdil[:], in_=mdil_d[:, :]), b_mdil, writes=[b_mdil])
    b_pre = Buf("precast")

    precast_items = []

    def precast(dst, src, keys):
        precast_items.append((dst, src, keys))
    for j0 in range(0, NJ, 4):
        precast(wg_c[1][j0:j0 + 4], wg_d[1][j0:j0 + 4], [("g", 1, j) for j in range(j0, j0 + 4)])
        precast(wu_c[1][j0:j0 + 4], wu_d[1][j0:j0 + 4], [("u", 1, j) for j in range(j0, j0 + 4)])
    for cb in range(4):
        for jg in range(0, 4, 2):
            precast(wd_c[1][cb, jg:jg + 2], wd_d[1][cb, jg:jg + 2], [(("d", 1), cb, jg), (("d", 1), cb, jg + 1)])
    for cb in range(0, 4, 2):
        precast(wo_c[cb:cb + 2, 0], wo_d[cb:cb + 2, 0], [("o", cb), ("o", cb + 1)])
        precast(wpg_c[cb:cb + 2, 0], wpg_d[cb:cb + 2, 0], [("pg", cb), ("pg", cb + 1)])
    precast(wpp_c[:, 0], wpp_d[:, 0], [("pp", cb) for cb in range(4)])

    def precast_some(n):
        for _ in range(n):
            if precast_items:
                dst, src, keys = precast_items.pop(0)
                P.dma("pool", lambda e, dst=dst, src=src: e.dma_start(out=dst, in_=src), b_pre,
                      writes=[cbuf(k) for k in keys])
    SCALE = float(128 ** -0.5)
    n_et = 0
    n_sb = 0
    n_yb = 0

    def load_head(h):
        s = h % 2
        P.dma("sp", lambda e: e.dma_start(out=qs_[s][:], in_=qT_d[h]), b_qs[s], reads=[b_qT[h]], writes=[b_qs[s]])
        P.dma("sp", lambda e: e.dma_start(out=ks_[s][:], in_=kT_d[h]), b_ks[s], reads=[b_kT[h]], writes=[b_ks[s]])
        P.dma("sp", lambda e: e.dma_start(out=vs_[s][:], in_=vT_d[h]), b_vs[s], reads=[b_vT[h]], writes=[b_vs[s]])

    def make_vt(s, fold, d, tile_list):
        per_res = 32 // d
        for g0 in range(0, len(tile_list), 8):
            grp = tile_list[g0:g0 + 8]
            bank = 6 + (g0 // 8) % 2
            pT = psb(bank, 1)
            for n, ti in enumerate(grp):
                r, mt = ti // per_res, ti % per_res
                start = r + d * 128 * mt
                src = vs_[s][:, start:start + d * 127 + 1:d]
                P.op("pe", lambda e, n=n, src=src, pT=pT: e.transpose(out=pT[:, n * 128:(n + 1) * 128], in_=src,
                                                                      identity=ident_b[:]),
                     reads=[b_vs[s], b_const], writes=[b_ps[bank]], inc=(n == len(grp) - 1))
            assert grp == list(range(grp[0], grp[0] + len(grp)))
            dst = vt[:, fold, grp[0]:grp[0] + len(grp), :]
            srcv = pT[:, 0:len(grp) * 128].rearrange("p (a b) -> p a b", b=128)
            if (g0 // 8) % 2 == 0:
                P.op("act", lambda e, dst=dst, srcv=srcv: e.copy(out=dst, in_=srcv), reads=[b_ps[bank]],
                     writes=[b_vt[fold]])
            else:
                P.op("dve", lambda e, dst=dst, srcv=srcv: e.tensor_copy(out=dst, in_=srcv), reads=[b_ps[bank]],
                     writes=[b_vt[fold]])

    def attn_group(s, q_ap, key_aps, mask_aps, vt_aps, mask_buf, fold, o_ap, first, esc=None):
        nonlocal n_et, n_sb, n_yb
        nt = len(key_aps)
        esc = SCALE if esc is None else esc
        sb0 = 2 * (n_sb % 2)
        n_sb += 1
        es = n_et % 3
        n_et += 1
        yb = 4 + n_yb % 2
        n_yb += 1
        sflat = ps[:, sb0:sb0 + 2, :].rearrange("p a b -> p (a b)")
        for j in range(nt):
            reg = sflat[:, j * 128:(j + 1) * 128]
            bank = sb0 + (j * 128) // 512
            P.op("pe", lambda e, reg=reg, m=mask_aps[j]: e.matmul(reg, lhsT=ident_b[:], rhs=m, start=True, stop=False),
                 reads=[b_const, mask_buf], writes=[b_ps[bank]], inc=False)
            P.op("pe", lambda e, reg=reg, k=key_aps[j]: e.matmul(reg, lhsT=k, rhs=q_ap, start=False, stop=True),
                 reads=[b_ks[s], b_qs[s]], writes=[b_ps[bank]], inc=(j == nt - 1 or j == 3))
        w0 = min(nt, 4) * 128
        P.op("act", lambda e: e.activation(out=et[es][:, 0:w0], in_=sflat[:, 0:w0], func=AF.Exp, scale=esc),
             reads=[b_ps[sb0]], writes=[b_et[es]])
        if nt > 4:
            P.op("act", lambda e: e.activation(out=et[es][:, w0:nt * 128], in_=sflat[:, w0:nt * 128], func=AF.Exp,
                                               scale=esc),
                 reads=[b_ps[sb0 + 1]], writes=[b_et[es]])
        def pv_part():
            for j in range(nt):
                P.op("pe", lambda e, j=j: e.matmul(ps[:, yb, 0:128], lhsT=vt_aps[j],
                                                   rhs=et[es][:, j * 128:(j + 1) * 128],
                                                   start=(j == 0), stop=(j == nt - 1)),
                     reads=[b_vt[fold], b_et[es]], writes=[b_ps[yb]], inc=False)
            for j in range(nt):
                P.op("pe", lambda e, j=j: e.matmul(ps[:, yb, 128:256], lhsT=ones_b[:],
                                                   rhs=et[es][:, j * 128:(j + 1) * 128],
                                                   start=(j == 0), stop=(j == nt - 1)),
                     reads=[b_const, b_et[es]], writes=[b_ps[yb]], inc=(j == nt - 1))
            yv = ps[:, yb, 0:256].rearrange("p (a b) -> p a b", a=2)
            if first:
                P.op("dve", lambda e: e.tensor_copy(out=o_ap, in_=yv), reads=[b_ps[yb]], writes=[b_oacc])
            else:
                P.op("dve", lambda e: e.tensor_tensor(out=o_ap, in0=yv, in1=o_ap, op=ALU.add),
                     reads=[b_ps[yb], b_oacc], writes=[b_oacc])
        pending.append(pv_part)
        if len(pending) > 1:
            pending.pop(0)()

    pending = []

    def attn_flush():
        while pending:
            pending.pop(0)()

    load_head(0)
    for h in range(16):
        s = h % 2
        if h + 1 < 16:
            load_head(h + 1)
        precast_some(3)
        if h < 4:
            P.dma("pool", lambda e, h=h: e.dma_start(out=natab[:], in_=natab_d[h]), b_natab, writes=[b_natab])
            make_vt(s, 0, 1, list(range(5, 27)))
            for j in range(16):
                var = 0 if j == 0 else 1 if j == 1 else 3 if j == 14 else 4 if j == 15 else 2
                q_ap = qs_[s][:, j * 128:(j + 1) * 128]
                kt0 = 5 + j
                key_aps = [ks_[s][:, (kt0 + n) * 128:(kt0 + n + 1) * 128] for n in range(7)]
                mask_aps = [natab[:, var * 896 + n * 128: var * 896 + (n + 1) * 128] for n in range(7)]
                vt_aps = [vt[:, 0, kt0 + n, :] for n in range(7)]
                o_ap = oacc[:, :, j * 128:(j + 1) * 128]
                attn_group(s, q_ap, key_aps, mask_aps, vt_aps, b_natab, 0, o_ap, True, esc=1.0)
        else:
            make_vt(s, 0, 1, list(range(7, 25)))
            for r in range(4):
                make_vt(s, 1, 4, list(range(r * 8 + 1, r * 8 + 7)))
            for r0 in range(0, 16, 4):
                make_vt(s, 2, 16, list(range(r0 * 2, r0 * 2 + 8)))
            for mt in range(8, 24):
                var = 0 if mt == 8 else 2 if mt == 23 else 1
                q_ap = qs_[s][:, (mt - 8) * 128:(mt - 7) * 128]
                key_aps = [ks_[s][:, (mt - 1 + n) * 128:(mt + n) * 128] for n in range(3)]
                mask_aps = [mdil[:, var * 384 + n * 128: var * 384 + (n + 1) * 128] for n in range(3)]
                vt_aps = [vt[:, 0, mt - 1 + n, :] for n in range(3)]
                o_ap = oacc[:, :, (mt - 8) * 128:(mt - 7) * 128]
                attn_group(s, q_ap, key_aps, mask_aps, vt_aps, b_mdil, 0, o_ap, True)
            for r in range(4):
                for mt in range(2, 6):
                    var = 0 if mt == 2 else 2 if mt == 5 else 1
                    q0 = r + 4 * 128 * mt - OWN0
                    q_ap = qs_[s][:, q0:q0 + 4 * 127 + 1:4]
                    key_aps = []
                    for n in range(3):
                        k0 = r + 4 * 128 * (mt - 1 + n)
                        key_aps.append(ks_[s][:, k0:k0 + 4 * 127 + 1:4])
                    mask_aps = [mdil[:, var * 384 + n * 128: var * 384 + (n + 1) * 128] for n in range(3)]
                    vt_aps = [vt[:, 1, r * 8 + mt - 1 + n, :] for n in range(3)]
                    o_ap = oacc[:, :, q0:q0 + 4 * 127 + 1:4]
                    attn_group(s, q_ap, key_aps, mask_aps, vt_aps, b_mdil, 1, o_ap, False)
            for r in range(16):
                q0 = r + 16 * 64 - OWN0
                q_ap = qs_[s][:, q0:q0 + 16 * 127 + 1:16]
                key_aps = []
                for n in range(2):
                    k0 = r + 16 * 128 * n
                    key_aps.append(ks_[s][:, k0:k0 + 16 * 127 + 1:16])
                mask_aps = [mdil[:, 1152 + n * 128: 1152 + (n + 1) * 128] for n in range(2)]
                vt_aps = [vt[:, 2, r * 2 + n, :] for n in range(2)]
                o_ap = oacc[:, :, q0:q0 + 16 * 127 + 1:16]
                attn_group(s, q_ap, key_aps, mask_aps, vt_aps, b_mdil, 2, o_ap, False)
        attn_flush()
        if h == 15:
            precast_some(1000)
        P.op("dve", lambda e: e.reciprocal(out=oacc[:, 1, :], in_=oacc[:, 1, :]), reads=[b_oacc], writes=[b_oacc])
        P.op("dve", lambda e: e.tensor_tensor(out=oacc[:, 0, :], in0=oacc[:, 0, :], in1=oacc[:, 1, :], op=ALU.mult),
             reads=[b_oacc], writes=[b_oacc])
        P.op("act", lambda e: e.activation(out=sq[:], in_=oacc[:, 0, :], func=AF.Square), reads=[b_oacc],
             writes=[b_sq])
        obs = h % 2
        for c4 in range(4):
            bank = 6 + c4 % 2
            P.op("pe", lambda e, c4=c4, bank=bank: e.matmul(ps[:, bank, :], lhsT=ones_b[:],
                                                            rhs=sq[:, c4 * 512:(c4 + 1) * 512], start=True, stop=True),
                 reads=[b_sq, b_const], writes=[b_ps[bank]])
            P.op("act", lambda e, c4=c4, bank=bank: e.activation(out=rs_t[:, c4 * 512:(c4 + 1) * 512],
                                                                 in_=ps[:, bank, :], func=AF.Sqrt,
                                                                 bias=eps_t[:, 0:1], scale=1.0 / 128),
                 reads=[b_ps[bank], b_eps], writes=[b_rs])
        P.op("dve", lambda e: e.reciprocal(out=rs_t[:], in_=rs_t[:]), reads=[b_rs], writes=[b_rs])
        P.op("dve", lambda e, h=h, obs=obs: e.scalar_tensor_tensor(out=ob[obs][:], in0=oacc[:, 0, :],
                                                                   scalar=outg[:, h:h + 1], in1=rs_t[:],
                                                                   op0=ALU.mult, op1=ALU.mult),
             reads=[b_oacc, b_outg, b_rs], writes=[b_ob[obs]])
        P.dma("sp", lambda e, h=h, obs=obs: e.dma_start(out=oT_d[h], in_=ob[obs][:]), b_ob[obs],
              reads=[b_ob[obs]], writes=[b_oT[h]])
    if stop_phase <= 3:
        return finish(P, nc, b_oT + b_h1)
    P.barrier()
    P.sb_reset(arena0)

    NT4 = NOWN // T
    u4T_d = dscr("u4Ts", [16, 128, NOWN], BF16)
    b_u4T = [Buf("u4T_%d" % i) for i in range(NT4)]

    class SB4:
        pass

    def alloc_small(ple):
        B = SB4()
        B.xin = [P.sb("xin", [128, D], F32) for _ in range(2)]
        B.b_xin = [Buf("xin%d" % i) for i in range(2)]
        B.ft = [P.sb("ft", [128, D], F32) for _ in range(4)]
        B.b_ft = [Buf("ft%d" % i) for i in range(4)]
        B.junk = P.sb("junk", [128, D], BF16)
        B.b_junk = Buf("junk")
        B.uTs = [P.sb("uT", [128, 16, T], BF16) for _ in range(2)]
        B.b_uTs = [[Buf("uT%d_%d" % (u, i)) for i in range(4)] for u in range(2)]
        B.wmv = [P.sb("wmv", [128, 16, 512], BF16) for _ in range(3)]
        B.b_wmv = [Buf("wmv%d" % i) for i in range(3)]
        B.gpost = P.sb("gpost", [128, D], F32)
        B.b_gpost = Buf("gpost")
        B.sg = [P.sb("sg", [128, 512], F32) for _ in range(2)]
        B.b_sg = [Buf("sg%d" % i) for i in range(2)]
        B.wmv_n = 0
        return B

    def proj_phase(B, ntiles, in_d, b_in, w_d, w_c, wkey, resid_fn, out_store, evac_fn=None, extra_fn=None):
        def load_u(t):
            ub = t % 2
            P.dma("sp", lambda e: e.dma_start(out=B.uTs[ub][:], in_=in_d[:, :, t * T:(t + 1) * T].rearrange(
                "h p t -> p h t")), B.b_uTs[ub][0], reads=b_in(t), writes=B.b_uTs[ub])

        def post_units(t):
            def Pst(i):
                ap, bsrc = resid_fn(t, i)
                post_residual(B, i, ap, bsrc, 1.0, si=i % 2)
                out_store(B, t, i)
            return [[lambda: Pst(0), lambda: Pst(1)], [lambda: Pst(2), lambda: Pst(3)]]
        if ntiles:
            load_u(0)
        pend = []
        for t in range(ntiles):
            ub = t % 2
            if t + 1 < ntiles:
                load_u(t + 1)
            if extra_fn is not None:
                extra_fn(B, t)
            for cb in range(4):
                s_ = B.wmv_n % 3
                B.wmv_n += 1
                wload(B.wmv[s_][:], B.b_wmv[s_], w_d[cb, 0], w_c[cb, 0], (wkey, cb), False)
                for i in range(4):
                    for k in range(16):
                        P.op("pe", lambda e, s_=s_, i=i, k=k, ub=ub: e.matmul(
                            ps[:, 4 + i, :], lhsT=B.uTs[ub][:, k, i * 128:(i + 1) * 128], rhs=B.wmv[s_][:, k, :],
                            start=(k == 0), stop=(k == 15)),
                            reads=[B.b_wmv[s_], B.b_uTs[ub][i]], writes=[b_ps[4 + i]], inc=(k == 15))
                    if cb == 0 and pend and i % 2 == 1:
                        run_unit(pend.pop(0))
                while cb == 0 and pend:
                    run_unit(pend.pop(0))
                for i in range(4):
                    if evac_fn is None:
                        evac_to_ft(B)(i, cb)
                    else:
                        evac_fn(B, t, i, cb)
            pend = post_units(t)
        while pend:
            run_unit(pend.pop(0))

    B = alloc_small(False)
    load_gain(B, "post", 3)
    proj_phase(B, NT4, oT_d, lambda t: b_oT, wo_d, wo_c, "o",
               lambda t, i: (h1_d[(t * 4 + i) * 128:(t * 4 + i + 1) * 128, :], b_h1[t * 4 + i]),
               lambda B, t, i: P.dma("sp", lambda e: e.dma_start(
                   out=h2_d[(t * 4 + i) * 128:(t * 4 + i + 1) * 128, :], in_=B.ft[i][:]),
                   B.b_ft[i], reads=[B.b_ft[i]], writes=[b_h2[t * 4 + i]]))
    if stop_phase <= 4:
        pass
    P.barrier()
    P.sb_reset(arena0)

    def p4_src(t, i):
        oi = t * 4 + i
        return h2_d[oi * 128:(oi + 1) * 128, :], b_h2[oi]

    def p4_post_store(B, t, i):
        oi = t * 4 + i
        P.dma("sp", lambda e: e.dma_start(out=h3_d[oi * 128:(oi + 1) * 128, :], in_=B.ft[i][:]),
              B.b_ft[i], reads=[B.b_ft[i]], writes=[b_h3[oi]])

    def p4_u4_store(B, t, ub):
        P.dma("sp", lambda e: e.dma_start(out=u4T_d[:, :, t * T:(t + 1) * T].rearrange("k p t -> p k t"),
                                          in_=B.uTs[ub][:]),
              B.b_uTs[ub][0], reads=B.b_uTs[ub], writes=[b_u4T[t]])

    ffn_pipeline(alloc_ffn_bufs(pipelined=True), 1, NT4, p4_src, 2, 5, p4_post_store, 3, p4_u4_store,
                 cast_first=False)
    P.barrier()
    P.sb_reset(arena0)

    B = alloc_small(True)
    pin = [P.sb("pin", [128, 256], F32) for _ in range(2)]
    b_pin = [Buf("pin%d" % i) for i in range(2)]
    pnb = P.sb("pnb", [128, 256], BF16)
    b_pnb = Buf("pnb")
    pTt = [P.sb("pTt", [128, 2, T], BF16) for _ in range(2)]
    b_pTt = [[Buf("pTt%d_%d" % (u, i)) for i in range(4)] for u in range(2)]
    wpp_s = [P.sb("wpp_s", [128, 2, 512], BF16) for _ in range(2)]
    b_wpp = [Buf("wpp%d" % i) for i in range(2)]
    load_gain(B, "post", 7)

    def ple_extra(B, t):
        for i in range(4):
            oi = t * 4 + i
            pi = i % 2
            P.dma("sp", lambda e, pi=pi, oi=oi: e.dma_start(out=pin[pi][:], in_=p_d[oi * 128:(oi + 1) * 128, :]),
                  b_pin[pi], writes=[b_pin[pi]])
            P.op("dve", lambda e, pi=pi: e.tensor_copy(out=pnb[:], in_=pin[pi][:]), reads=[b_pin[pi]], writes=[b_pnb])
            pT = psb(2, 1)
            for c in range(2):
                P.op("pe", lambda e, c=c, pT=pT: e.transpose(out=pT[:, c * 128:(c + 1) * 128],
                                                             in_=pnb[:, c * 128:(c + 1) * 128], identity=ident_b[:]),
                     reads=[b_pnb, b_const], writes=[b_ps[2]], inc=(c == 1))
            P.op("act", lambda e, i=i, pT=pT, t=t: e.copy(out=pTt[t % 2][:, :, i * 128:(i + 1) * 128],
                                                          in_=pT[:, 0:256].rearrange("p (c t) -> p c t", c=2)),
                 reads=[b_ps[2]], writes=[b_pTt[t % 2][i]])

    def ple_evac(B, t, i, cb):
        ws = cb % 2
        pb = i % 2
        if i == 0:
            wload(wpp_s[ws][:], b_wpp[ws], wpp_d[cb, 0], wpp_c[cb, 0], ("pp", cb), False)
        for c in range(2):
            P.op("pe", lambda e, c=c: e.matmul(ps[:, pb, :], lhsT=pTt[t % 2][:, c, i * 128:(i + 1) * 128],
                                               rhs=wpp_s[ws][:, c, :], start=(c == 0), stop=(c == 1)),
                 reads=[b_pTt[t % 2][i], b_wpp[ws]], writes=[b_ps[pb]], inc=(c == 1))
        sgs = i % 2
        P.op("act", lambda e: e.activation(out=B.sg[sgs][:], in_=ps[:, 4 + i, :], func=AF.Sigmoid),
             reads=[b_ps[4 + i]], writes=[B.b_sg[sgs]])
        P.op("dve", lambda e: e.tensor_tensor(out=B.ft[i][:, cb * 512:(cb + 1) * 512], in0=ps[:, pb, :],
                                              in1=B.sg[sgs][:], op=ALU.mult),
             reads=[b_ps[pb], B.b_sg[sgs]], writes=[B.b_ft[i]])

    proj_phase(B, NT4, u4T_d, lambda t: [b_u4T[t]], wpg_d, wpg_c, "pg",
               lambda t, i: (h3_d[(t * 4 + i) * 128:(t * 4 + i + 1) * 128, :], b_h3[t * 4 + i]),
               lambda B, t, i: P.dma("sp", lambda e: e.dma_start(
                   out=out_d[(t * 4 + i) * 128:(t * 4 + i + 1) * 128, :], in_=B.ft[i][:]),
                   B.b_ft[i], reads=[B.b_ft[i]], writes=[b_out[t * 4 + i]]),
               evac_fn=ple_evac, extra_fn=ple_extra)
    return finish(P, nc, b_out)


def finish(P, nc, bufs):
    P.barrier()
    print('[kernel] semaphores used:', getattr(P, 'nsem', 0))
    P.emit()
    P.close()
    return nc


def lay_stat(W):
    K, Fo = W.shape
    return np.ascontiguousarray(W.reshape(K // 128, 128, Fo // 128, 128).transpose(2, 1, 0, 3))


def lay_mov(W, kg):
    K, Do = W.shape
    nj = K // 128
    return np.ascontiguousarray(W.reshape(nj // kg, kg, 128, Do // 512, 512).transpose(3, 0, 2, 1, 4))


def dil_masks(q):
    k = np.arange(128)[:, None]
    qq = np.arange(128)[None, :]
    tabs = []
    for var in range(3):
        t = np.zeros((128, 384), np.float32)
        for j in range(3):
            ok = np.abs(128 * (j - 1) + k - qq) <= 64
            if var == 0 and j == 0 and q == 0:
                ok = np.zeros_like(ok)
            if var == 2 and j == 2 and q == 3:
                ok = np.zeros_like(ok)
            t[:, j * 128:(j + 1) * 128] = np.where(ok, 0.0, MASKV)
        tabs.append(t)
    t = np.zeros((128, 256), np.float32)
    for j in range(2):
        mk = 128 * j + k
        mq = 64 + qq
        ok = np.abs(mk - mq) <= 64
        if q == 0:
            ok = ok & (mk >= 64)
        if q == 3:
            ok = ok & (mk < 192)
        t[:, j * 128:(j + 1) * 128] = np.where(ok, 0.0, MASKV)
    tabs.append(t)
    return np.ascontiguousarray(np.concatenate(tabs, axis=1))


def na_tables(q, rpb):
    out = np.full((4, 128, 5, 7, 128), MASKV, np.float32)
    jsel = [0, 1, 7, 14, 15]
    kk = np.arange(128)
    k_ro, k_c = kk // 64, kk % 64
    qq = np.arange(128)
    q_ro, q_c = qq // 64, qq % 64
    for vi, j in enumerate(jsel):
        r0 = 32 * q + 2 * j
        rg = r0 + q_ro
        rs = np.clip(rg - 4, 0, 128 - 8)
        qs = np.clip(q_c - 8, 0, 64 - 16)
        for n in range(7):
            kr = r0 - 6 + 2 * n + k_ro
            ok = (kr[:, None] >= rs[None, :]) & (kr[:, None] < rs[None, :] + 8) & \
                 (k_c[:, None] >= qs[None, :]) & (k_c[:, None] < qs[None, :] + 16) & \
                 (kr[:, None] >= 0) & (kr[:, None] < 128)
            dr = np.clip(kr[:, None] - rg[None, :] + 7, 0, 14)
            dc = np.clip(k_c[:, None] - q_c[None, :] + 15, 0, 30)
            for h in range(4):
                out[h, :, vi, n, :] = np.where(ok, rpb[h][dr, dc], MASKV)
    return np.ascontiguousarray(out.reshape(4, 128, 5 * 896))


def rope_tables(q):
    g0 = q * NOWN - OWN0
    inv = np.float32(10000.0) ** (-np.arange(0, 128, 2, dtype=np.float32) / np.float32(128))
    s = (g0 + np.arange(NTOK)).astype(np.float32)
    ang = (s[:, None] * inv[None, :]).astype(np.float32)
    c = np.cos(ang).astype(np.float32).T
    sn = np.sin(ang).astype(np.float32).T
    return (np.ascontiguousarray(np.concatenate([c, c], 0)), np.ascontiguousarray(np.concatenate([sn, sn], 0)))


def make_consts():
    ident = np.eye(128, dtype=np.float32)
    rot = np.zeros((128, 128), np.float32)
    for m in range(64):
        rot[m + 64, m] = -1.0
        rot[m, m + 64] = 1.0
    ones = np.ones((128, 128), np.float32)
    return np.stack([ident, rot, ones])


def prep_inputs(inputs, cores=range(8)):
    f = lambda a: np.asarray(a, dtype=np.float32)
    x = f(inputs["x"])
    p = f(inputs["p"])[0]
    shared = {
        "wg1": lay_stat(f(inputs["ffn1_w_gate"])[0]), "wu1": lay_stat(f(inputs["ffn1_w_up"])[0]),
        "wd1": lay_mov(f(inputs["ffn1_w_down"])[0], 11),
        "wg2": lay_stat(f(inputs["ffn2_w_gate"])[0]), "wu2": lay_stat(f(inputs["ffn2_w_up"])[0]),
        "wd2": lay_mov(f(inputs["ffn2_w_down"])[0], 11),
        "wqkv": lay_stat(f(inputs["w_qkv"])[0]),
        "wo": lay_mov(f(inputs["w_o"])[0], 16),
        "wpg": lay_mov(f(inputs["w_ple_gate"])[0], 16),
        "wpp": lay_mov(f(inputs["w_ple_proj"])[0], 2),
        "gains": np.ascontiguousarray(np.stack([f(inputs[k])[0] for k in (
            "ffn1_pre_g", "ffn1_post_g", "mix_pre_g", "mix_post_g", "ffn2_pre_g", "ffn2_post_g",
            "ple_pre_g", "ple_post_g")])),
        "outg": np.ascontiguousarray(f(inputs["out_g"])[0].reshape(16, 128).T),
        "gainsT": np.ascontiguousarray(np.concatenate([f(inputs[k])[0].reshape(16, 128).T for k in (
            "ffn1_pre_g", "mix_pre_g", "ffn2_pre_g", "ple_pre_g")], axis=1)),
        "consts": make_consts(),
    }
    rpb = f(inputs["na_rpb"])[0]
    in_maps = []
    for c in cores:
        b, q = c // 4, c % 4
        g0 = q * NOWN - OWN0
        xh = np.zeros((NTOK, D), np.float32)
        lo, hi = max(g0, 0), min(g0 + NTOK, SEQ)
        xh[lo - g0:hi - g0] = x[b, lo:hi]
        cs, sn = rope_tables(q)
        m = dict(shared)
        m.update({"x": xh, "p": np.ascontiguousarray(p[b, q * NOWN:(q + 1) * NOWN]),
                  "cos": cs, "sin": sn, "mdil": dil_masks(q), "natab": na_tables(q, rpb)})
        in_maps.append(m)
    return in_maps


_NC_CACHE = {}


def kernel(**inputs):
    in_maps = prep_inputs(inputs)
    if "nc" not in _NC_CACHE:
        _NC_CACHE["nc"] = build_program()
    res = run_bass_kernel_spmd(_NC_CACHE["nc"], in_maps, core_ids=list(range(8)))
    out = np.zeros((2, SEQ, D), np.float32)
    for c in range(8):
        b, q = c // 4, c % 4
        out[b, q * NOWN:(q + 1) * NOWN] = res.results[c]["out"]
    return out
```
